# Optimizing a Trainium2 kernel written in Bass

```python
import jax, jax.numpy as jnp
from jax import lax
import numpy as np

D_MODEL = 1024
BATCH = 8
SEQ = 4096
DEPTH = 1

CONV_CH = D_MODEL
CONV_K = 31
HEAD_DIM = 128
N_HEADS = D_MODEL // HEAD_DIM
DELTA_W = N_HEADS * HEAD_DIM
SHORT_CONV = 5
N_DIR = 2
CHUNK = 64
D_FF = -(-8 * D_MODEL // (3 * 256)) * 256
PLE_DIM = 256
NORM_EPS = 1e-6
LN_EPS = 1e-5
IN_SIZES = (CONV_CH, CONV_CH, DELTA_W, DELTA_W, DELTA_W, DELTA_W,
            N_DIR * N_HEADS, N_DIR * N_HEADS, D_MODEL, D_MODEL)
IN_TOTAL = sum(IN_SIZES)

kernel_name = "hybrid_conformer_gdn_parallel_block"


def rms_norm(x, g):
    xf = x.astype(jnp.float32)
    y = xf * lax.rsqrt(jnp.mean(xf * xf, axis=-1, keepdims=True) + NORM_EPS)
    return (y * g.astype(jnp.float32)).astype(x.dtype)


def layer_norm(x, g, b):
    xf = x.astype(jnp.float32)
    mu = jnp.mean(xf, axis=-1, keepdims=True)
    xc = xf - mu
    y = xc * lax.rsqrt(jnp.mean(xc * xc, axis=-1, keepdims=True) + LN_EPS)
    return (y * g.astype(jnp.float32) + b.astype(jnp.float32)).astype(x.dtype)


def l2_norm(x):
    xf = x.astype(jnp.float32)
    return xf * lax.rsqrt(jnp.sum(xf * xf, axis=-1, keepdims=True) + NORM_EPS)


def depthwise_conv(x, w):
    k = w.shape[0]
    return lax.conv_general_dilated(
        x, w[:, None, :].astype(x.dtype), window_strides=(1,),
        padding=[(k // 2, k // 2)], dimension_numbers=("NWC", "WIO", "NWC"),
        feature_group_count=x.shape[-1])


def split_cols(t, sizes):
    out, start = [], 0
    for s in sizes:
        out.append(t[..., start:start + s])
        start += s
    return out


def gated_delta_rule(q, k, v, g, beta):
    out_dtype = v.dtype
    bsz, t_len, n_h, _ = q.shape
    n_blk = t_len // CHUNK

    def blocks(t):
        t = t.astype(jnp.float32).reshape((bsz, n_blk, CHUNK, n_h) + t.shape[3:])
        return jnp.moveaxis(t, 3, 1)

    q, k, v, g, beta = (blocks(t) for t in (q, k, v, g, beta))
    g = jnp.cumsum(g, axis=-1)
    idx = jnp.arange(CHUNK)
    tri_incl = idx[:, None] >= idx[None, :]
    tri_strict = idx[:, None] > idx[None, :]
    decay = jnp.exp(jnp.where(tri_incl, g[..., :, None] - g[..., None, :], -jnp.inf))
    k_beta = k * beta[..., None]
    lower = jnp.where(tri_strict, jnp.einsum('bhncd,bhnsd->bhncs', k_beta, k) * decay, 0.0)
    dv = v.shape[-1]
    rhs = jnp.concatenate([v * beta[..., None], k_beta * jnp.exp(g)[..., None]], axis=-1)
    sol = lax.linalg.triangular_solve(lower, rhs, left_side=True, lower=True, unit_diagonal=True)
    u, w = sol[..., :dv], sol[..., dv:]
    qk = jnp.einsum('bhncd,bhnsd->bhncs', q, k) * decay
    g_last = g[..., -1]
    q_dec = q * jnp.exp(g)[..., None]
    k_tail = k * jnp.exp(g_last[..., None] - g)[..., None]

    def step(state, xs):
        qd, kt, ui, wi, qki, gl = xs
        v_new = ui - jnp.einsum('bhck,bhkv->bhcv', wi, state)
        o = jnp.einsum('bhck,bhkv->bhcv', qd, state) + jnp.einsum('bhcs,bhsv->bhcv', qki, v_new)
        state = state * jnp.exp(gl)[..., None, None] + jnp.einsum('bhck,bhcv->bhkv', kt, v_new)
        return state, o

    xs = tuple(jnp.moveaxis(t, 2, 0) for t in (q_dec, k_tail, u, w, qk, g_last))
    s0 = jnp.zeros((bsz, n_h, q.shape[-1], dv), jnp.float32)
    _, o = lax.scan(step, s0, xs)
    o = jnp.transpose(o, (1, 0, 3, 2, 4)).reshape(bsz, t_len, n_h, dv)
    return o.astype(out_dtype)


def setup_inputs(seed: int = 0) -> dict:
    key = jax.random.key(seed)
    ks = jax.random.split(key, 32)
    f32 = jnp.float32

    def nrm(k, shape, fan_in):
        return jax.random.normal(k, shape, f32) * (fan_in ** -0.5)

    def gain(k, shape):
        return 1.0 + 0.05 * jax.random.normal(k, shape, f32)

    def small(k, shape):
        return 0.01 * jax.random.normal(k, shape, f32)

    a_log = jnp.log(jax.random.uniform(ks[10], (DEPTH, N_DIR, N_HEADS), f32, 1.0, 16.0))
    dt = jnp.exp(jax.random.uniform(ks[11], (DEPTH, N_DIR, N_HEADS), f32,
                                    float(np.log(1e-3)), float(np.log(1e-1))))
    dt_bias = dt + jnp.log(-jnp.expm1(-dt))
    return {
        "x": jax.random.normal(ks[0], (BATCH, SEQ, D_MODEL), f32),
        "p": jax.random.normal(ks[1], (DEPTH, BATCH, SEQ, PLE_DIM), f32),
        "g_mix": gain(ks[2], (DEPTH, D_MODEL)),
        "w_in": nrm(ks[3], (DEPTH, D_MODEL, IN_TOTAL), D_MODEL),
        "conv_dw_w": nrm(ks[4], (DEPTH, CONV_K, CONV_CH), CONV_K),
        "conv_dw_b": small(ks[5], (DEPTH, CONV_CH)),
        "conv_ln_g": gain(ks[6], (DEPTH, CONV_CH)),
        "conv_ln_b": small(ks[7], (DEPTH, CONV_CH)),
        "w_conv_out": nrm(ks[8], (DEPTH, CONV_CH, D_MODEL), CONV_CH),
        "qkv_conv_w": nrm(ks[9], (DEPTH, SHORT_CONV, 3 * DELTA_W), SHORT_CONV),
        "a_log": a_log,
        "dt_bias": dt_bias,
        "delta_norm_g": gain(ks[12], (DEPTH, HEAD_DIM)),
        "w_delta_out": nrm(ks[13], (DEPTH, DELTA_W, D_MODEL), DELTA_W),
        "w_o": nrm(ks[14], (DEPTH, D_MODEL, D_MODEL), D_MODEL),
        "g_ffn": gain(ks[15], (DEPTH, D_MODEL)),
        "w_gate_up": nrm(ks[16], (DEPTH, D_MODEL, 2 * D_FF), D_MODEL),
        "w_down": nrm(ks[17], (DEPTH, D_FF, D_MODEL), D_FF),
        "g_pl": gain(ks[18], (DEPTH, D_MODEL)),
        "w_pl_gate": nrm(ks[19], (DEPTH, D_MODEL, D_MODEL), D_MODEL),
        "w_pl_proj": nrm(ks[20], (DEPTH, PLE_DIM, D_MODEL), PLE_DIM),
        "g_pl_proj": gain(ks[21], (DEPTH, D_MODEL)),
        "g_final": gain(ks[22], (D_MODEL,)),
    }


def reference(x, p, g_mix, w_in, conv_dw_w, conv_dw_b, conv_ln_g, conv_ln_b, w_conv_out,
              qkv_conv_w, a_log, dt_bias, delta_norm_g, w_delta_out, w_o, g_ffn,
              w_gate_up, w_down, g_pl, w_pl_gate, w_pl_proj, g_pl_proj, g_final):
    bsz, t_len, _ = x.shape
    for i in range(DEPTH):
        h = rms_norm(x, g_mix[i])
        proj = h @ w_in[i]
        glu_a, glu_b, q, k, v, z, a_in, b_in, gate_a, gate_b = split_cols(proj, IN_SIZES)

        c = glu_a * jax.nn.sigmoid(glu_b)
        c = depthwise_conv(c, conv_dw_w[i]) + conv_dw_b[i]
        c = jax.nn.silu(layer_norm(c, conv_ln_g[i], conv_ln_b[i]))
        y_conv = c @ w_conv_out[i]

        qkv = jax.nn.silu(depthwise_conv(jnp.concatenate([q, k, v], axis=-1), qkv_conv_w[i]))
        q, k, v = split_cols(qkv, (DELTA_W, DELTA_W, DELTA_W))
        q = l2_norm(q.reshape(bsz, t_len, N_HEADS, HEAD_DIM)) * (HEAD_DIM ** -0.5)
        k = l2_norm(k.reshape(bsz, t_len, N_HEADS, HEAD_DIM))
        v = v.reshape(bsz, t_len, N_HEADS, HEAD_DIM)
        a_in = a_in.astype(jnp.float32).reshape(bsz, t_len, N_DIR, N_HEADS)
        b_in = b_in.astype(jnp.float32).reshape(bsz, t_len, N_DIR, N_HEADS)
        g = -jnp.exp(a_log[i].astype(jnp.float32)) * jax.nn.softplus(a_in + dt_bias[i].astype(jnp.float32))
        beta = jax.nn.sigmoid(b_in)
        o_fwd = gated_delta_rule(q, k, v, g[:, :, 0], beta[:, :, 0])
        rev = lambda t: jnp.flip(t, axis=1)
        o_bwd = rev(gated_delta_rule(rev(q), rev(k), rev(v), rev(g[:, :, 1]), rev(beta[:, :, 1])))
        o = rms_norm(o_fwd + o_bwd, delta_norm_g[i]) * jax.nn.silu(z.reshape(bsz, t_len, N_HEADS, HEAD_DIM))
        y_delta = o.reshape(bsz, t_len, DELTA_W).astype(x.dtype) @ w_delta_out[i]

        y = jax.nn.sigmoid(gate_a) * y_conv + jax.nn.sigmoid(gate_b) * y_delta
        x = x + y @ w_o[i]

        h = rms_norm(x, g_ffn[i])
        gt, up = split_cols(h @ w_gate_up[i], (D_FF, D_FF))
        x = x + (jax.nn.silu(gt) * up) @ w_down[i]

        pl = rms_norm(p[i] @ w_pl_proj[i], g_pl_proj[i])
        x = x + jax.nn.sigmoid(rms_norm(x, g_pl[i]) @ w_pl_gate[i]) * pl
    return rms_norm(x, g_final)
```

```python
import contextlib
import numpy as np
import concourse.bass as bass
import concourse.mybir as mybir
from concourse.bass_utils import run_bass_kernel_spmd

F32 = mybir.dt.float32
BF16 = mybir.dt.bfloat16
AF = mybir.ActivationFunctionType
ALU = mybir.AluOpType
AX = mybir.AxisListType

T = 4096
D = 1024
NH = 8
NCH = 32
DFF = 2816
NFB = 22
PLE = 256
CK = 31
EPS = 1e-6
LN_EPS = 1e-5

O_GLUA, O_GLUB, O_Q, O_K, O_V, O_Z, O_A, O_B, O_GA, O_GB = 0, 1024, 2048, 3072, 4096, 5120, 6144, 6160, 6176, 7200

C_ID, C_TL, C_TG, C_ONE, C_SGF, C_SGB, C_NEG, C_MUP, C_MLO, C_LV = 0, 128, 256, 384, 512, 640, 768, 896, 1024, 1152
C_LV6X = C_LV + 7 * 256
C_LVQ = C_LV6X + 256
POOL_LV = (3,)
NCONST = C_LVQ + len(POOL_LV) * 512
P_GMIX, P_GFFN, P_GPL, P_CB, P_LNG, P_LNB, P_QKVW, P_CW = 0, 8, 16, 24, 32, 40, 48, 168
P_DNG = P_CW + 8 * CK
NP = P_DNG + 1
R_DTB, R_ALOG, R_DNG, R_GPLP, R_GFIN = 0, 16, 32, 160, 1184
NR = R_GFIN + 1024


class Buf:
    __slots__ = ("name", "writer", "readers", "dsem", "dcnt")

    def __init__(self, name):
        self.name = name
        self.writer = None
        self.readers = {}
        self.dsem = None
        self.dcnt = 0


class KB:
    def __init__(self):
        self.nc = bass.Bass("TRN2", target_bir_lowering=False)
        nc = self.nc
        self.engs = {"pe": nc.tensor, "act": nc.scalar, "dve": nc.vector,
                     "pool": nc.gpsimd, "sp": nc.sync}
        self.sems = {}
        self.cnt = {}
        for e in ("pe", "act", "dve", "pool"):
            self.sems[e] = nc.alloc_semaphore("s_" + e)
            self.cnt[e] = 0
        self.waited = {}
        self.dbufs = []
        self.nops = 0
        self.root = contextlib.ExitStack()

    def sb(self, name, shape, dt, st=None):
        return (st or self.root).enter_context(self.nc.sbuf_tensor(name, list(shape), dt))

    def _wait(self, eng, tok):
        if tok is None:
            return
        key, val = tok
        if val <= self.waited.get((eng, key), 0):
            return
        self.engs[eng].wait_ge(self.sems[key], val)
        self.waited[(eng, key)] = val

    def _deps(self, eng, reads, writes):
        for b in reads:
            self._wait(eng, b.writer)
        for b in writes:
            self._wait(eng, b.writer)
            for k, v in list(b.readers.items()):
                self._wait(eng, (k, v))

    def _commit(self, tok, reads, writes):
        k, v = tok
        for b in reads:
            if b.readers.get(k, 0) < v:
                b.readers[k] = v
        for b in writes:
            b.writer = tok
            b.readers = {}

    def op(self, eng, fn, reads=(), writes=()):
        self._deps(eng, reads, writes)
        ins = fn(self.engs[eng])
        self.cnt[eng] += 1
        ins.then_inc(self.sems[eng], 1)
        tok = (eng, self.cnt[eng])
        self._commit(tok, reads, writes)
        self.nops += 1
        return tok

    def group(self, eng, fns, reads=(), writes=()):
        self._deps(eng, reads, writes)
        ins = None
        for fn in fns:
            ins = fn(self.engs[eng])
            self.nops += 1
        self.cnt[eng] += 1
        ins.then_inc(self.sems[eng], 1)
        tok = (eng, self.cnt[eng])
        self._commit(tok, reads, writes)
        return tok

    def dma(self, q, out, in_, reads=(), writes=(), **kw):
        self._deps(q, reads, writes)
        b = writes[0]
        if b.dsem is None:
            b.dsem = "d%d" % len(self.dbufs)
            self.dbufs.append(b)
            self.sems[b.dsem] = self.nc.alloc_semaphore(b.dsem)
        ins = self.engs[q].dma_start(out=out, in_=in_, **kw)
        ins.then_inc(self.sems[b.dsem], 16)
        b.dcnt += 16
        tok = (b.dsem, b.dcnt)
        self._commit(tok, reads, writes)
        self.nops += 1
        return tok

    def barrier(self):
        for e in ("pe", "act", "dve", "pool", "sp"):
            for k in ("pe", "act", "dve", "pool"):
                if self.cnt[k] > 0:
                    self._wait(e, (k, self.cnt[k]))
            for b in self.dbufs:
                if b.dcnt > 0:
                    self._wait(e, (b.dsem, b.dcnt))


def mm(out, lhsT, rhs, start=True, stop=True):
    return lambda e: e.matmul(out, lhsT, rhs, start=start, stop=stop)


def build(dbg=False, phases=(0, 1, 2, 3, 4, 5, 6), nheads=NH):
    kb = KB()
    nc = kb.nc

    def din(name, shape, dt=F32):
        return nc.dram_tensor(name, list(shape), dt, kind="ExternalInput").ap()

    x_d = din("x", [T, D])
    p_d = din("p", [T, PLE])
    wh_d = din("w_heads", [NH, 128, 8, 512])
    wab_d = din("w_ab", [128, 8, 32])
    wglu_d = din("w_glu", [128, 8, 2048])
    wgate_d = din("w_gate", [128, 8, 2048])
    wco_d = din("w_conv_out", [128, 8, D])
    wdo_d = din("w_delta_out", [128, 8, D])
    wo_d = din("w_o", [128, 8, D])
    wgu_d = din("w_gate_up", [128, 8, 2 * DFF])
    wdn_d = din("w_down", [128, NFB, D])
    wplg_d = din("w_pl_gate", [128, 8, D])
    wplp_d = din("w_pl_proj", [128, 2, D])
    consts_d = din("consts", [128, NCONST])
    pvec_d = din("pvec", [128, NP])
    rvec_d = din("rvec", [1, NR])
    out_d = nc.dram_tensor("out", [T, D], F32, kind="ExternalOutput").ap()
    oT_d = nc.dram_tensor("oT_s", [NH, 128, T], BF16, kind="Internal").ap()
    cT_d = nc.dram_tensor("cT_s", [8, 128, T + 30], BF16, kind="Internal").ap()
    ycT_d = nc.dram_tensor("ycT_s", [8, 128, T], BF16, kind="Internal").ap()
    x1_d = nc.dram_tensor("x1_s", [T, D], F32, kind="Internal").ap()
    gT_d = nc.dram_tensor("gT_s", [16, 128, T], BF16, kind="Internal").ap()
    x2_d = nc.dram_tensor("x2_s", [T, D], F32, kind="Internal").ap()
    hs_d = nc.dram_tensor("hs_s", [NH, 5, 128, T], BF16, kind="Internal").ap()
    hsdb = [Buf("hs_d%d" % i) for i in range(NH)]
    oTb = [Buf("oT_d%d" % i) for i in range(NH)]
    cTb = Buf("cT_d")
    ycTb = Buf("ycT_d")
    x1b = Buf("x1_d")
    x2b = Buf("x2_d")
    gTb = Buf("gT_d")
    outb = Buf("out_d")

    dbg_outs = {}

    def dump(name, ap, shape, dt, rb):
        if not dbg:
            return
        o = nc.dram_tensor("dbg_" + name, list(shape), dt, kind="ExternalOutput").ap()
        b = Buf("dbg_" + name)
        kb.dma("sp", o, ap, reads=rb, writes=[b])
        dbg_outs[name] = b

    banks = [nc.alloc_psum_tensor("bank%d" % i, [128, 512], F32) for i in range(8)]
    bb = [Buf("bank%d" % i) for i in range(8)]

    def bank16(i, lo, hi):
        return banks[i][:, lo:hi].bitcast(BF16)

    cf = kb.sb("cf", [128, C_MUP], F32)
    cb = kb.sb("cb", [128, NCONST], BF16)
    pv = kb.sb("pv", [128, NP], F32)
    epsc = kb.sb("epsc", [128, 4], F32)
    cfb, cbb, pvb, epsb = Buf("cf"), Buf("cb"), Buf("pv"), Buf("eps")
    NSTG = 2
    stg = [kb.sb("stg%d" % i, [128, 512], F32) for i in range(NSTG)]
    stgb = [Buf("stg%d" % i) for i in range(NSTG)]
    stg_i = [0]

    kb.dma("sp", cf[:], consts_d[:, 0:C_MUP], writes=[cfb])
    kb.dma("sp", pv[:], pvec_d, writes=[pvb])
    kb.op("pool", lambda e: e.memset(epsc[:, 0:1], EPS), writes=[epsb])
    kb.op("pool", lambda e: e.memset(epsc[:, 1:2], 128.0 * EPS), writes=[epsb])
    kb.op("pool", lambda e: e.memset(epsc[:, 2:3], LN_EPS), writes=[epsb])
    kb.op("pool", lambda e: e.memset(epsc[:, 3:4], -0.5), writes=[epsb])
    for n0 in range(0, NCONST, 512):
        n1 = min(NCONST, n0 + 512)
        s = stg_i[0] % len(stg)
        stg_i[0] += 1
        kb.dma("sp", stg[s][:, 0:n1 - n0], consts_d[:, n0:n1], writes=[stgb[s]])
        kb.op("pool", lambda e: e.tensor_copy(cb[:, n0:n1], stg[s][:, 0:n1 - n0]), reads=[stgb[s]], writes=[cbb])
    identb = cb[:, C_ID:C_ID + 128]
    onesb = cb[:, C_ONE:C_ONE + 128]

    lc_i = [0]

    def drop_extra_stg():
        del stg[NSTG:]
        del stgb[NSTG:]

    def extra_stg(st, n, tag):
        drop_extra_stg()
        for i in range(n):
            stg.append(kb.sb("stgx_%s%d" % (tag, i), [128, 512], F32, st))
            stgb.append(Buf("stgx_%s%d" % (tag, i)))


    def load_cast(dst, dstb, src, KC, N, scale_col=None, scale_const=None, eng=None, order=None):
        nchunk = (N + 511) // 512
        for ci in (order if order is not None else range(nchunk)):
            n0 = ci * 512
            n1 = min(N, n0 + 512)
            db = dstb[ci] if isinstance(dstb, list) else dstb
            for kc in range(KC):
                s = stg_i[0] % len(stg)
                stg_i[0] += 1
                kb.dma("sp", stg[s][:, 0:n1 - n0], src[:, kc, n0:n1], writes=[stgb[s]])
                o = dst[:, kc, n0:n1]
                i_ = stg[s][:, 0:n1 - n0]
                if eng is None:
                    eng_ = ("act", "dve")[lc_i[0] % 2]
                    lc_i[0] += 1
                else:
                    eng_ = eng
                if eng_ == "act":
                    assert not (scale_col is not None and scale_const is not None)
                    if scale_col is not None:
                        sc = pv[:, scale_col + kc:scale_col + kc + 1]
                        kb.op("act", lambda e: e.activation(o, i_, AF.Copy, scale=sc), reads=[stgb[s], pvb], writes=[db])
                    elif scale_const is not None:
                        kb.op("act", lambda e: e.activation(o, i_, AF.Copy, scale=float(scale_const)), reads=[stgb[s]], writes=[db])
                    else:
                        kb.op("act", lambda e: e.copy(o, i_), reads=[stgb[s]], writes=[db])
                    continue
                if scale_col is not None:
                    sc = pv[:, scale_col + kc:scale_col + kc + 1]
                    if scale_const is not None:
                        kb.op(eng_, lambda e: e.tensor_scalar(o, i_, sc, float(scale_const), ALU.mult, ALU.mult),
                              reads=[stgb[s], pvb], writes=[db])
                    else:
                        kb.op(eng_, lambda e: e.tensor_scalar(o, i_, sc, None, ALU.mult),
                              reads=[stgb[s], pvb], writes=[db])
                elif scale_const is not None:
                    kb.op(eng_, lambda e: e.tensor_scalar(o, i_, float(scale_const), None, ALU.mult),
                          reads=[stgb[s]], writes=[db])
                else:
                    kb.op(eng_, lambda e: e.tensor_copy(o, i_), reads=[stgb[s]], writes=[db])

    gall = kb.sb("gall", [128, NCH, 16], F32)
    ball = kb.sb("ball", [128, NCH, 16], F32)

    stA = contextlib.ExitStack()
    hT = kb.sb("hT", [128, 8, T], BF16, stA)
    hTb = [Buf("hT%d" % i) for i in range(8)]

    def rms_to_featmajor(src_tile, src_b, dstT, dst_b, col0, st_tiles, it, part=None):
        junk, junkb, stat, statb, xn, xnb, bk = st_tiles
        c3 = 3 * it
        if part != "b":
            rms_part_a(src_tile, src_b, junk, junkb, stat, statb, xn, xnb, c3)
        if part != "a":
            rms_part_b(dstT, dst_b, col0, xn, xnb, bk, it)

    def rms_part_a(src_tile, src_b, junk, junkb, stat, statb, xn, xnb, c3):
        kb.op("act", lambda e: e.activation(junk[:], src_tile, AF.Square, scale=1.0 / 32.0,
                                            accum_out=stat[:, c3:c3 + 1]),
              reads=[src_b], writes=[junkb, statb])
        kb.op("act", lambda e: e.activation(stat[:, c3 + 1:c3 + 2], stat[:, c3:c3 + 1], AF.Ln,
                                            bias=epsc[:, 0:1]), reads=[epsb], writes=[statb])
        kb.op("act", lambda e: e.activation(stat[:, c3 + 2:c3 + 3], stat[:, c3 + 1:c3 + 2], AF.Exp,
                                            scale=-0.5), writes=[statb])
        kb.op("dve", lambda e: e.tensor_scalar(xn[:], src_tile, stat[:, c3 + 2:c3 + 3], None, ALU.mult),
              reads=[src_b, statb], writes=[xnb])

    def rms_part_b(dstT, dst_b, col0, xn, xnb, bk, it):
        pv16 = bank16(bk, 0, 512)
        kb.group("pe", [(lambda e, c=c: e.transpose(pv16[:, c * 128:(c + 1) * 128], xn[:, c * 128:(c + 1) * 128], identb))
                        for c in range(8)], reads=[xnb, cbb], writes=[bb[bk]])
        if it % 2 == 0:
            kb.op("act", lambda e: e.copy(dstT[:, :, col0:col0 + 128], pv16.rearrange("p (c t) -> p c t", c=8)),
                  reads=[], writes=[bb[bk], dst_b])
        else:
            kb.op("pool", lambda e: e.tensor_copy(xn[:], xn[:]), reads=[], writes=[]) if False else None
            kb.op("dve", lambda e: e.tensor_copy(dstT[:, :, col0:col0 + 128], pv16.rearrange("p (c t) -> p c t", c=8)),
                  reads=[], writes=[bb[bk], dst_b])

    if 0 in phases:
        with contextlib.ExitStack() as st0:
            xst = [kb.sb("xst%d" % i, [128, D], F32, st0) for i in range(2)]
            xstb = [Buf("xst%d" % i) for i in range(2)]
            xn = [kb.sb("xn%d" % i, [128, D], BF16, st0) for i in range(2)]
            xnb = [Buf("xn%d" % i) for i in range(2)]
            junk = [kb.sb("junk0_%d" % i, [128, D], BF16, st0) for i in range(2)]
            junkb = [Buf("junk0_%d" % i) for i in range(2)]
            stat = [kb.sb("stat0_%d" % i, [128, 3 * NCH], F32, st0) for i in range(2)]
            statb = [Buf("stat0_%d" % i) for i in range(2)]
            for i in range(2):
                kb.op("pool", lambda e: e.memset(stat[i][:], 0.0), writes=[statb[i]])
            def p0(tt, part):
                s = tt % 2
                if part == "a":
                    kb.dma("sp", xst[s][:], x_d[tt * 128:(tt + 1) * 128, :], writes=[xstb[s]])
                rms_to_featmajor(xst[s][:], xstb[s], hT, hTb[tt // 4], tt * 128,
                                 (junk[s], junkb[s], stat[s], statb[s], xn[s], xnb[s], tt % 2), tt, part=part)
            p0(0, "a")
            for tt in range(NCH):
                if tt + 1 < NCH:
                    p0(tt + 1, "a")
                p0(tt, "b")
            kb.barrier()
        if dbg:
            dump("hT", hT[:, :, 0:512], [128, 8, 512], BF16, [hTb[0]])

    if 1 in phases:
        with contextlib.ExitStack() as stF:
            rv = kb.sb("rv1", [128, 160], F32, stF)
            rvb = Buf("rv1")
            kb.dma("sp", rv[:], rvec_d[:, 0:160].partition_broadcast(128), writes=[rvb])
            wab = kb.sb("wab", [128, 8, 32], BF16, stF)
            wabb = Buf("wab")
            load_cast(wab, wabb, wab_d, 8, 32, scale_col=P_GMIX)
            gallb, ballb = Buf("gall"), Buf("ball")
            nega = kb.sb("nega", [128, 16], F32, stF)
            negab = Buf("nega")
            kb.op("act", lambda e: e.activation(nega[:], rv[:, R_ALOG:R_ALOG + 16], AF.Exp), reads=[rvb], writes=[negab])
            kb.op("dve", lambda e: e.tensor_scalar(nega[:], nega[:], -1.0, None, ALU.mult), writes=[negab])
            for half in range(2):
                bk = half
                fns = []
                for t16 in range(16):
                    tt = half * 16 + t16
                    for dc in range(8):
                        fns.append(mm(banks[bk][:, t16 * 32:(t16 + 1) * 32], hT[:, dc, tt * 128:(tt + 1) * 128],
                                      wab[:, dc, :], start=(dc == 0), stop=(dc == 7)))
                kb.group("pe", fns, reads=[wabb] + hTb, writes=[bb[bk]])
                ab3 = banks[bk][:, :].rearrange("p (t c) -> p t c", c=32)
                gsl = gall[:, half * 16:(half + 1) * 16, :]
                bsl = ball[:, half * 16:(half + 1) * 16, :]
                dtb_bc = rv[:, R_DTB:R_DTB + 16].unsqueeze(1).to_broadcast([128, 16, 16])
                nega_bc = nega[:, :].unsqueeze(1).to_broadcast([128, 16, 16])
                kb.op("dve", lambda e: e.tensor_tensor(gsl, ab3[:, :, 0:16], dtb_bc, ALU.add),
                      reads=[rvb], writes=[bb[bk], gallb])
                kb.op("act", lambda e: e.activation(bsl, ab3[:, :, 16:32], AF.Tanh, scale=0.5),
                      writes=[bb[bk], ballb])
                kb.op("act", lambda e: e.activation(gsl, gsl, AF.Exp), writes=[gallb])
                kb.op("act", lambda e: e.activation(gsl, gsl, AF.Ln, bias=1.0), writes=[gallb])
                kb.op("dve", lambda e: e.tensor_tensor(gsl, gsl, nega_bc, ALU.mult), reads=[negab], writes=[gallb])
                kb.op("dve", lambda e: e.tensor_scalar(bsl, bsl, 0.5, 0.5, ALU.mult, ALU.add), writes=[ballb])
            if dbg:
                dump("gall", gall[:], [128, NCH, 16], F32, [gallb])
                dump("ball", ball[:], [128, NCH, 16], F32, [ballb])


            wh2 = [kb.sb("wh%d" % i, [128, 8, 512], BF16, stF) for i in range(2)]
            wh2b = [Buf("wh%d" % i) for i in range(2)]
            pre = kb.sb("pre", [128, T + 4], BF16, stF)
            preb = Buf("pre")
            kb.op("pool", lambda e: e.memset(pre[:, 0:2], 0.0), writes=[preb])
            kb.op("pool", lambda e: e.memset(pre[:, T + 2:T + 4], 0.0), writes=[preb])
            dg = kb.sb("dg", [128, 5, 128], BF16, stF)
            dgb = Buf("dg")
            rawv = [kb.sb("rawv%d" % i, [128, 512], BF16, stF) for i in range(2)]
            rawvb = [Buf("rawv%d" % i) for i in range(2)]
            sq_ = [kb.sb("sq%d" % i, [128, 512], BF16, stF) for i in range(2)]
            sqb_ = [Buf("sq%d" % i) for i in range(2)]
            rs_ = [kb.sb("rs%d" % i, [128, 512], F32, stF) for i in range(2)]
            rsb_ = [Buf("rs%d" % i) for i in range(2)]
            hb5 = [[kb.sb("hb%d_%d" % (k, i), [128, T], BF16, stF) for k in range(5)] for i in range(2)]
            hb5b = [[Buf("hb%d_%d" % (k, i)) for k in range(5)] for i in range(2)]

            for h in range(nheads):
                hp = h % 2
                QT, KT = hb5[hp][0], hb5[hp][1]
                QTb, KTb = hb5b[hp][0], hb5b[hp][1]
                Ktok = hb5[hp][2][:, :].rearrange("p (c k) -> p c k", c=NCH)
                Vtok = hb5[hp][3][:, :].rearrange("p (c k) -> p c k", c=NCH)
                Zs = hb5[hp][4][:, :].rearrange("p (c k) -> p c k", c=NCH)
                Ktokb, Vtokb, Zsb = hb5b[hp][2], hb5b[hp][3], hb5b[hp][4]
                wh, whb = wh2[hp], wh2b[hp]
                load_cast(wh, whb, wh_d[h], 8, 512, scale_col=P_GMIX, eng="dve")
                ZsT = hb5[hp][4]
                for t8 in range(8):
                    bk = 6 + t8 % 2
                    kb.group("pe", [mm(banks[bk][:, :], wh[:, dc, 384:512], hT[:, dc, t8 * 512:(t8 + 1) * 512],
                                       start=(dc == 0), stop=(dc == 7)) for dc in range(8)], reads=[whb, hTb[t8]], writes=[bb[bk]])
                    kb.op("act", lambda e: e.activation(ZsT[:, t8 * 512:(t8 + 1) * 512], banks[bk][:, :], AF.Silu),
                          writes=[bb[bk], Zsb])

                def proj_tile(j, t8):
                    bk = 6 + t8 % 2
                    kb.group("pe", [mm(banks[bk][:, :], wh[:, dc, j * 128:(j + 1) * 128],
                                       hT[:, dc, t8 * 512:(t8 + 1) * 512], start=(dc == 0), stop=(dc == 7))
                                    for dc in range(8)], reads=[whb, hTb[t8]], writes=[bb[bk]])
                    kb.op("dve", lambda e: e.tensor_copy(pre[:, 2 + t8 * 512:2 + (t8 + 1) * 512], banks[bk][:, :]),
                          writes=[bb[bk], preb])

                def norm_a(j, t8):
                    dstT, dstb = ((QT, QTb), (KT, KTb))[j]
                    sq, sqb = sq_[t8 % 2], sqb_[t8 % 2]
                    rawt = dstT[:, t8 * 512:(t8 + 1) * 512]
                    kb.op("pool", lambda e: e.tensor_tensor(sq[:], rawt, rawt, ALU.mult), reads=[dstb], writes=[sqb])

                def norm_b(j, t8):
                    dstT, dstb = ((QT, QTb), (KT, KTb))[j]
                    sq, sqb, rs, rsb = sq_[t8 % 2], sqb_[t8 % 2], rs_[t8 % 2], rsb_[t8 % 2]
                    rawt = dstT[:, t8 * 512:(t8 + 1) * 512]
                    bk2 = 4 + (t8 % 2)
                    kb.op("pe", mm(banks[bk2][:, :], onesb, sq[:]), reads=[cbb, sqb], writes=[bb[bk2]])
                    if j == 0:
                        kb.op("act", lambda e: e.activation(rs[:], banks[bk2][:, :], AF.Ln, bias=epsc[:, 1:2], scale=128.0),
                              reads=[epsb], writes=[bb[bk2], rsb])
                    else:
                        kb.op("act", lambda e: e.activation(rs[:], banks[bk2][:, :], AF.Ln, bias=epsc[:, 0:1]),
                              reads=[epsb], writes=[bb[bk2], rsb])
                    kb.op("act", lambda e: e.activation(rs[:], rs[:], AF.Exp, scale=-0.5), writes=[rsb])
                    kb.op("dve", lambda e: e.tensor_tensor(rawt, rawt, rs[:], ALU.mult), reads=[rsb], writes=[dstb])

                for j in range(3):
                    blk = j * 8 + h
                    if j >= 1:
                        norm_a(j - 1, 0)
                    for t8 in range(8):
                        proj_tile(j, t8)
                        if j >= 1:
                            if t8 + 1 < 8:
                                norm_a(j - 1, t8 + 1)
                            norm_b(j - 1, t8)
                    kb.op("pool", lambda e: e.tensor_tensor(dg[:], identb.unsqueeze(1).to_broadcast([128, 5, 128]),
                                                            pv[:, P_QKVW + blk * 5:P_QKVW + blk * 5 + 5].unsqueeze(2).to_broadcast([128, 5, 128]),
                                                            ALU.mult), reads=[cbb, pvb], writes=[dgb])
                    for t8 in range(8):
                        bk = 6 + t8 % 2
                        kb.group("pe", [mm(banks[bk][:, :], dg[:, jj, :], pre[:, t8 * 512 + jj:t8 * 512 + jj + 512],
                                           start=(jj == 0), stop=(jj == 4)) for jj in range(5)],
                                 reads=[dgb, preb], writes=[bb[bk]])
                        if j < 2:
                            dstT, dstb = ((QT, QTb), (KT, KTb))[j]
                            kb.op("act", lambda e: e.activation(dstT[:, t8 * 512:(t8 + 1) * 512], banks[bk][:, :], AF.Silu),
                                  writes=[bb[bk], dstb])
                        else:
                            s_ = t8 % 2
                            kb.op("act", lambda e: e.activation(rawv[s_][:], banks[bk][:, :], AF.Silu), writes=[bb[bk], rawvb[s_]])

                            def vtrans(tv):
                                sv = tv % 2
                                bk2 = 2 + tv % 2
                                v16 = bank16(bk2, 0, 256)
                                kb.group("pe", [(lambda e, jx=jx: e.transpose(v16[:, jx * 128:(jx + 1) * 128], rawv[sv][:, jx * 128:(jx + 1) * 128], identb))
                                                for jx in range(4)], reads=[rawvb[sv], cbb], writes=[bb[bk2]])
                                kb.op("dve", lambda e: e.tensor_copy(Vtok[:, tv * 4:(tv + 1) * 4, :], v16.rearrange("p (j c) -> p j c", j=4)),
                                      writes=[bb[bk2], Vtokb])
                            if t8 >= 1:
                                vtrans(t8 - 1)
                            if t8 == 7:
                                vtrans(7)

                for c8 in range(4):
                    bk = 6 + c8 % 2
                    v16 = bank16(bk, 0, 512)
                    kb.group("pe", [(lambda e, jx=jx: e.transpose(v16[:, jx * 128:(jx + 1) * 128],
                                                                  KT[:, (c8 * 8 + jx) * 128:(c8 * 8 + jx + 1) * 128], identb))
                                    for jx in range(8)], reads=[KTb, cbb], writes=[bb[bk]])
                    kb.op("act", lambda e: e.copy(Ktok[:, c8 * 8:(c8 + 1) * 8, :], v16.rearrange("p (j c) -> p j c", j=8)),
                          writes=[bb[bk], Ktokb])
                if dbg and h == 0:
                    dump("QT", QT[:], [128, T], BF16, [QTb])
                    dump("KT", KT[:], [128, T], BF16, [KTb])
                    dump("Vtok", Vtok[:], [128, NCH, 128], BF16, [Vtokb])
                    dump("Ktok", Ktok[:], [128, NCH, 128], BF16, [Ktokb])

                for k in range(5):
                    kb.dma("pool", hs_d[h, k], hb5[hp][k][:], reads=[hb5b[hp][k]], writes=[hsdb[h]])
            kb.barrier()
    if 2 in phases:
        with contextlib.ExitStack() as st2:
            extra_stg(st2, 6, "p2")
            wglu = kb.sb("wglu", [128, 8, 2048], BF16, st2)
            wglub = [Buf("wglu%d" % i) for i in range(4)]
            load_cast(wglu, wglub, wglu_d, 8, 2048, scale_col=P_GMIX, order=[0, 2, 1, 3])
            crow = [kb.sb("crow%d" % i, [128, T + 30], BF16, st2) for i in range(2)]
            crowb = [Buf("crow%d" % i) for i in range(2)]
            tb_ = [kb.sb("tbg%d" % i, [128, 512], F32, st2) for i in range(2)]
            tbb = [Buf("tbg%d" % i) for i in range(2)]
            for i in range(2):
                kb.op("pool", lambda e: e.memset(crow[i][:, 0:15], 0.0), writes=[crowb[i]])
                kb.op("pool", lambda e: e.memset(crow[i][:, T + 15:T + 30], 0.0), writes=[crowb[i]])
            n = 0
            for cbk in range(8):
                r = cbk % 2
                for t8 in range(8):
                    s = n % 2
                    ba, bg = 2 * s, 2 * s + 1
                    n += 1
                    kb.group("pe", [mm(banks[ba][:, :], wglu[:, dc, cbk * 128:(cbk + 1) * 128],
                                       hT[:, dc, t8 * 512:(t8 + 1) * 512], start=(dc == 0), stop=(dc == 7)) for dc in range(8)],
                             reads=[wglub[cbk // 4], hTb[t8]], writes=[bb[ba]])
                    kb.group("pe", [mm(banks[bg][:, :], wglu[:, dc, 1024 + cbk * 128:1024 + (cbk + 1) * 128],
                                       hT[:, dc, t8 * 512:(t8 + 1) * 512], start=(dc == 0), stop=(dc == 7)) for dc in range(8)],
                             reads=[wglub[2 + cbk // 4], hTb[t8]], writes=[bb[bg]])
                    kb.op("act", lambda e: e.activation(tb_[s][:], banks[bg][:, :], AF.Sigmoid),
                          writes=[bb[bg], tbb[s]])
                    kb.op("dve", lambda e: e.tensor_tensor(crow[r][:, 15 + t8 * 512:15 + (t8 + 1) * 512], banks[ba][:, :],
                                                           tb_[s][:], ALU.mult),
                          reads=[tbb[s]], writes=[bb[ba], crowb[r]])
                kb.dma("pool", cT_d[cbk], crow[r][:], reads=[crowb[r]], writes=[cTb])
            wgate = kb.sb("wgate", [128, 8, 2048], BF16, st2)
            wgateb = [Buf("wgate%d" % i) for i in range(4)]
            load_cast(wgate, wgateb, wgate_d, 8, 2048, scale_col=P_GMIX)
            grow = [kb.sb("grow%d" % i, [128, T], BF16, st2) for i in range(2)]
            growb = [Buf("grow%d" % i) for i in range(2)]
            for gblk in range(16):
                r = gblk % 2
                for t8 in range(8):
                    bk = 4 + (t8 % 4)
                    kb.group("pe", [mm(banks[bk][:, :], wgate[:, dc, gblk * 128:(gblk + 1) * 128],
                                       hT[:, dc, t8 * 512:(t8 + 1) * 512], start=(dc == 0), stop=(dc == 7)) for dc in range(8)],
                             reads=[wgateb[gblk // 4], hTb[t8]], writes=[bb[bk]])
                    kb.op("act", lambda e: e.activation(grow[r][:, t8 * 512:(t8 + 1) * 512], banks[bk][:, :], AF.Sigmoid),
                          writes=[bb[bk], growb[r]])
                kb.dma("pool", gT_d[gblk], grow[r][:], reads=[growb[r]], writes=[gTb])
            kb.barrier()
    stA.close()
    if 1 in phases:
        drop_extra_stg()
        with contextlib.ExitStack() as st1:
            pre = kb.sb("preL", [128, T + 4], BF16, st1)
            preb = Buf("pre")
            kb.op("pool", lambda e: e.memset(pre[:, 0:2], 0.0), writes=[preb])
            kb.op("pool", lambda e: e.memset(pre[:, T + 2:T + 4], 0.0), writes=[preb])
            oacc2 = [kb.sb("oacc_%d" % j, [128, NCH, 128], F32, st1) for j in range(2)]
            oaccb2 = [[Buf("oacc%d_%d" % (j, i)) for i in range(NCH)] for j in range(2)]
            tabs2 = [{}, {}]
            for j in range(2):
                for nm in ("gsel", "bsel", "gc", "ngc", "egc", "cw", "ktl", "egl"):
                    tabs2[j][nm] = kb.sb("t_%s%d" % (nm, j), [128, NCH, 2], F32, st1)
            hsb2 = [Buf("headscal%d" % j) for j in range(2)]
            ssq = kb.sb("ssq", [128, 2 * NCH], F32, st1)
            ssqb = Buf("ssq")
            NSLOT = 6
            def mk(nm, shape, dt):
                return [kb.sb("%s_%d" % (nm, q), shape, dt, st1) for q in range(NSLOT)]
            def mkb(nm):
                return [Buf("%s_%d" % (nm, q)) for q in range(NSLOT)]
            G2_ = [kb.sb("G2_%d" % i, [128, 2, 128], F32, st1) for i in range(2)] * 3
            G2_b = [Buf("G2_%d" % i) for i in range(2)] * 3
            Gt2_ = [kb.sb("Gt2_%d" % i, [128, 2, 128], F32, st1) for i in range(2)] * 3
            Gt2_b = [Buf("Gt2_%d" % i) for i in range(2)] * 3
            nE2_ = [kb.sb("nE2_%d" % i, [128, 256], F32, st1) for i in range(2)] * 3
            nE2_b = [Buf("nE2_%d" % i) for i in range(2)] * 3
            F2_, F2_b = mk("F2", [128, 256], BF16), mkb("F2")
            Fb2_, Fb2_b = mk("Fb2", [128, 256], BF16), mkb("Fb2")
            Fm2_, Fm2_b = mk("Fm2", [128, 256], BF16), mkb("Fm2")
            AA4_, AA4_b = mk("AA4", [128, 512], BF16), mkb("AA4")
            qk2_, qk2_b = mk("qk2", [128, 256], BF16), mkb("qk2")
            rv2_, rv2_b = mk("rv2", [128, 256], BF16), mkb("rv2")
            rw2_, rw2_b = mk("rw2", [128, 256], BF16), mkb("rw2")
            kt2_, kt2_b = mk("kt2", [128, 256], BF16), mkb("kt2")
            nZ4_, nZ4_b = mk("nZ4", [128, 512], BF16), mkb("nZ4")
            TR4_, TR4_b = mk("TR4", [128, 512], BF16), mkb("TR4")
            Lm4_, Lm4_b = mk("Lm4", [128, 512], BF16), mkb("Lm4")
            nwT = [kb.sb("nwT%d" % d, [128, 128], BF16, st1) for d in range(2)]
            qs_ = [kb.sb("qs%d" % d, [128, 128], BF16, st1) for d in range(2)]
            vn = [kb.sb("vn%d" % d, [128, 128], BF16, st1) for d in range(2)]
            nwTb = [Buf("nwT%d" % d) for d in range(2)]
            qsb = [Buf("qs%d" % d) for d in range(2)]
            vnb = [Buf("vn%d" % d) for d in range(2)]
            S32 = [kb.sb("S32_%d" % d, [128, 128], F32, st1) for d in range(2)]
            S16 = [kb.sb("S16_%d" % d, [128, 128], BF16, st1) for d in range(2)]
            S32b = [Buf("S32_%d" % d) for d in range(2)]
            S16b = [Buf("S16_%d" % d) for d in range(2)]
            tri2 = cf[:, C_TL:C_TL + 256].rearrange("p (a c) -> p a c", a=2)
            ones2 = cf[:, C_ONE:C_ONE + 128].unsqueeze(1).to_broadcast([128, 2, 128])
            sg2 = cf[:, C_SGF:C_SGF + 256]
            negone = cf[:, C_NEG:C_NEG + 128]
            mask2 = cb[:, C_MUP:C_MUP + 256]


            hb5 = [[kb.sb("lb%d_%d" % (k, i), [128, T], BF16, st1) for k in range(5)] for i in range(2)]
            hb5b = [[Buf("lb%d_%d" % (k, i)) for k in range(5)] for i in range(2)]

            def load_head(hh):
                for k in range(5):
                    kb.dma("sp", hb5[hh % 2][k][:], hs_d[hh, k], reads=[hsdb[hh]], writes=[hb5b[hh % 2][k]])

            def make_head(h):
                hp = h % 2
                tabs, hsb = tabs2[hp], hsb2[hp]
                oacc, oaccb = oacc2[hp], oaccb2[hp]
                QT, KT = hb5[hp][0], hb5[hp][1]
                QTb, KTb = hb5b[hp][0], hb5b[hp][1]
                Ktok = hb5[hp][2][:, :].rearrange("p (c k) -> p c k", c=NCH)
                Vtok = hb5[hp][3][:, :].rearrange("p (c k) -> p c k", c=NCH)
                Zs = hb5[hp][4][:, :].rearrange("p (c k) -> p c k", c=NCH)
                Ktokb, Vtokb, Zsb = hb5b[hp][2], hb5b[hp][3], hb5b[hp][4]
                og, ogb = Ktok, Ktokb

                def prologue(bk):
                    for d in range(2):
                        col = d * 8 + h
                        gsrc = gall[:, :, col] if d == 0 else gall[:, ::-1, col]
                        bsrc = ball[:, :, col] if d == 0 else ball[:, ::-1, col]
                        kb.op("pool", lambda e: e.tensor_copy(tabs["gsel"][:, :, d], gsrc), reads=[gallb], writes=[hsb])
                        kb.op("pool", lambda e: e.tensor_copy(tabs["bsel"][:, :, d], bsrc), reads=[ballb], writes=[hsb])
                    kb.group("pe", [
                        mm(banks[bk][:, 0:32], cf[:, C_TL:C_TL + 128], tabs["gsel"][:, :, 0]),
                        mm(banks[bk][:, 32:64], cf[:, C_TG:C_TG + 128], tabs["gsel"][:, :, 1]),
                        mm(banks[bk][:, 64:96], cf[:, C_ONE:C_ONE + 128], tabs["gsel"][:, :, 0]),
                        mm(banks[bk][:, 96:128], cf[:, C_ONE:C_ONE + 128], tabs["gsel"][:, :, 1]),
                    ], reads=[hsb, cfb], writes=[bb[bk]])
                    for d in range(2):
                        gcp = banks[bk][:, d * 32:(d + 1) * 32]
                        glp = banks[bk][:, 64 + d * 32:64 + (d + 1) * 32]
                        kb.op("dve", lambda e: e.tensor_copy(tabs["gc"][:, :, d], gcp), writes=[bb[bk], hsb])
                        kb.op("act", lambda e: e.activation(tabs["egc"][:, :, d], gcp, AF.Exp), writes=[bb[bk], hsb])
                        kb.op("dve", lambda e: e.tensor_tensor(tabs["ktl"][:, :, d], glp, tabs["gc"][:, :, d], ALU.subtract),
                              writes=[bb[bk], hsb])
                        kb.op("act", lambda e: e.activation(tabs["egl"][:, :, d], glp, AF.Exp), writes=[bb[bk], hsb])
                    kb.op("dve", lambda e: e.tensor_tensor(tabs["cw"][:], tabs["egc"][:], tabs["bsel"][:], ALU.mult), writes=[hsb])
                    kb.op("dve", lambda e: e.tensor_scalar(tabs["ngc"][:], tabs["gc"][:], -1.0, None, ALU.mult), writes=[hsb])
                    kb.op("act", lambda e: e.activation(tabs["ktl"][:], tabs["ktl"][:], AF.Exp), writes=[hsb])


                def state_init():
                    for d in range(2):
                        kb.op("pool", lambda e: e.memset(S32[d][:], 0.0), writes=[S32b[d]])
                        kb.op("pool", lambda e: e.memset(S16[d][:], 0.0), writes=[S16b[d]])


                def pair_stages(it, q):
                    pbk = q
                    cc = (it, NCH - 1 - it)
                    G2, G2b = G2_[q], G2_b[q]
                    Gt2, Gt2b = Gt2_[q], Gt2_b[q]
                    nE2, nE2b = nE2_[q], nE2_b[q]
                    F2, F2b = F2_[q], F2_b[q]
                    Fb2, Fb2b = Fb2_[q], Fb2_b[q]
                    Fm2, Fm2b = Fm2_[q], Fm2_b[q]
                    AA4, AA4b = AA4_[q], AA4_b[q]
                    qk2, qk2b = qk2_[q], qk2_b[q]
                    rv2, rv2b = rv2_[q], rv2_b[q]
                    rw2, rw2b = rw2_[q], rw2_b[q]
                    kt2, kt2b = kt2_[q], kt2_b[q]
                    nZ4, nZ4b = nZ4_[q], nZ4_b[q]
                    TR4, TR4b = TR4_[q], TR4_b[q]
                    Lm4, Lm4b = Lm4_[q], Lm4_b[q]
                    pbank = banks[pbk]
                    bsc = lambda nm: tabs[nm][:, it, :].unsqueeze(2).to_broadcast([128, 2, 128])
                    L_op = (AA4[:, 0:128], AA4[:, 384:512])
                    LT_op = (AA4[:, 256:384], AA4[:, 128:256])
                    st = []

                    def p0():
                        kb.op("pool", lambda e: e.tensor_tensor(G2[:], cf[:, C_ID:C_ID + 128].unsqueeze(1).to_broadcast([128, 2, 128]),
                                                                bsc("gc"), ALU.mult), reads=[cfb, hsb], writes=[G2b])
                        fns = [mm(pbank[:, 0:256], cf[:, C_ONE:C_ONE + 128], G2[:, :, :].rearrange("p a c -> p (a c)"))]
                        for x in range(2):
                            c0 = cc[x] * 128
                            fns.append(mm(pbank[:, 256 + x * 128:256 + (x + 1) * 128], KT[:, c0:c0 + 128], KT[:, c0:c0 + 128]))
                        kb.group("pe", fns, reads=[G2b, cfb, KTb], writes=[bb[pbk]])
                    st.append(p0)

                    def p1():
                        for x in range(2):
                            kb.op("act", lambda e: e.activation(nE2[:, x * 128:(x + 1) * 128], pbank[:, x * 128:(x + 1) * 128], AF.Abs,
                                                                bias=tabs["ngc"][:, it, x:x + 1]),
                                  reads=[hsb], writes=[bb[pbk], nE2b])
                    st.append(p1)

                    def p2():
                        kb.op("act", lambda e: e.activation(F2[:], nE2[:], AF.Exp, scale=-1.0), reads=[nE2b], writes=[F2b])
                    st.append(p2)

                    def p3():
                        f3 = F2[:, :].rearrange("p (a c) -> p a c", a=2)
                        kb.op("pool", lambda e: e.tensor_tensor(Fb2[:, :].rearrange("p (a c) -> p a c", a=2), f3, bsc("bsel"), ALU.mult),
                              reads=[F2b, hsb], writes=[Fb2b])
                        kb.op("pool", lambda e: e.tensor_tensor(Fm2[:], F2[:], mask2, ALU.mult), reads=[F2b, cbb], writes=[Fm2b])
                        for x in range(2):
                            c = cc[x]
                            xs = slice(x * 128, (x + 1) * 128)
                            bc1 = lambda nm: tabs[nm][:, it, x:x + 1].to_broadcast([128, 128])
                            kb.op("pool", lambda e: e.tensor_tensor(rv2[:, xs], Vtok[:, c, :], bc1("bsel"), ALU.mult),
                                  reads=[Vtokb, hsb], writes=[rv2b])
                            kb.op("pool", lambda e: e.tensor_tensor(rw2[:, xs], Ktok[:, c, :], bc1("cw"), ALU.mult),
                                  reads=[Ktokb, hsb], writes=[rw2b])
                            kb.op("pool", lambda e: e.tensor_tensor(kt2[:, xs], Ktok[:, c, :], bc1("ktl"), ALU.mult),
                                  reads=[Ktokb, hsb], writes=[kt2b])
                    st.append(p3)

                    def p4():
                        kb.op("dve", lambda e: e.tensor_tensor(AA4[:, 0:256], pbank[:, 256:512], Fb2[:], ALU.mult),
                              reads=[Fb2b], writes=[bb[pbk], AA4b])
                    st.append(p4)

                    def p5():
                        at16 = bank16(pbk, 256, 384)
                        fns = []
                        for x in range(2):
                            c0 = cc[x] * 128
                            fns.append(mm(pbank[:, x * 128:(x + 1) * 128], KT[:, c0:c0 + 128], QT[:, c0:c0 + 128]))
                            fns.append(lambda e, x=x: e.transpose(at16[:, x * 128:(x + 1) * 128], AA4[:, x * 128:(x + 1) * 128], identb))
                        kb.group("pe", fns, reads=[KTb, QTb, AA4b, cbb], writes=[bb[pbk]])
                    st.append(p5)

                    def p6():
                        kb.op("dve", lambda e: e.tensor_tensor(qk2[:], pbank[:, 0:256], Fm2[:], ALU.mult), reads=[Fm2b], writes=[bb[pbk], qk2b])
                        kb.op("act", lambda e: e.copy(AA4[:, 256:512], bank16(pbk, 256, 384)), writes=[bb[pbk], AA4b])
                    st.append(p6)

                    ACT_LEVELS = (2, 4)
                    for l in range(7):
                        lv = cb[:, C_LV + l * 256:C_LV + (l + 1) * 256]
                        Tc = (identb, identb) if l == 0 else (TR4[:, 0:128], TR4[:, 256:384])
                        Rc = (identb, identb) if l == 0 else (TR4[:, 128:256], TR4[:, 384:512])
                        rd = [AA4b, cbb] + ([] if l == 0 else [TR4b])
                        if l < 6:
                            if l == 0:
                                def za():
                                    pass
                                def zb(lv=lv):
                                    aa = AA4[:, :].rearrange("p (b c) -> p b c", b=4)
                                    nz = nZ4[:, :].rearrange("p (b c) -> p b c", b=4)
                                    lvm = lv.rearrange("p (b c) -> p b c", b=2)
                                    kb.op("pool", lambda e: e.tensor_tensor(nz[:, 0:2, :], aa[:, 0:4:2, :], lvm, ALU.mult),
                                          reads=[AA4b, cbb], writes=[nZ4b])
                                    kb.op("pool", lambda e: e.tensor_tensor(nz[:, 2:4, :], aa[:, 3:0:-2, :], lvm, ALU.mult),
                                          reads=[AA4b, cbb], writes=[nZ4b])
                            elif l in POOL_LV:
                                def za(Tc=Tc, Rc=Rc, rd=rd):
                                    LmT = (Lm4[:, 256:384], Lm4[:, 128:256])
                                    Lm = (Lm4[:, 0:128], Lm4[:, 384:512])
                                    fns = []
                                    for x in range(2):
                                        fns.append(mm(pbank[:, x * 256:x * 256 + 128], LmT[x], Tc[x]))
                                        fns.append(mm(pbank[:, x * 256 + 128:x * 256 + 256], Lm[x], Rc[x]))
                                    kb.group("pe", fns, reads=[Lm4b, TR4b], writes=[bb[pbk]])
                                def zb(lv=lv):
                                    kb.op("act", lambda e: e.copy(nZ4[:], pbank[:, :]), writes=[bb[pbk], nZ4b])
                            else:
                                def za(Tc=Tc, Rc=Rc, rd=rd):
                                    fns = []
                                    for x in range(2):
                                        fns.append(mm(pbank[:, x * 256:x * 256 + 128], LT_op[x], Tc[x]))
                                        fns.append(mm(pbank[:, x * 256 + 128:x * 256 + 256], L_op[x], Rc[x]))
                                    kb.group("pe", fns, reads=rd, writes=[bb[pbk]])
                                def zb(lv=lv):
                                    kb.op("dve", lambda e: e.tensor_tensor(nZ4[:, :].rearrange("p (a c) -> p a c", a=2),
                                                                           pbank[:, :].rearrange("p (a c) -> p a c", a=2),
                                                                           lv.unsqueeze(1).to_broadcast([128, 2, 256]), ALU.mult),
                                          reads=[cbb], writes=[bb[pbk], nZ4b])
                            def zc(Tc=Tc, Rc=Rc, rd=rd, l=l):
                                if l == 0:
                                    return
                                fns = []
                                for x in range(2):
                                    o = x * 256
                                    if l in ACT_LEVELS:
                                        fns += [mm(pbank[:, o:o + 128], Rc[x], identb, start=True, stop=False),
                                                mm(pbank[:, o:o + 128], Rc[x], nZ4[:, o:o + 128], start=False, stop=True),
                                                mm(pbank[:, o + 128:o + 256], Tc[x], identb, start=True, stop=False),
                                                mm(pbank[:, o + 128:o + 256], Tc[x], nZ4[:, o + 128:o + 256], start=False, stop=True)]
                                    else:
                                        fns += [mm(pbank[:, o:o + 128], Rc[x], nZ4[:, o:o + 128]),
                                                mm(pbank[:, o + 128:o + 256], Tc[x], nZ4[:, o + 128:o + 256])]
                                kb.group("pe", fns, reads=rd + [nZ4b], writes=[bb[pbk]])
                                if (l + 1) in POOL_LV:
                                    lvq = cb[:, C_LVQ + POOL_LV.index(l + 1) * 512:C_LVQ + (POOL_LV.index(l + 1) + 1) * 512]
                                    kb.op("pool", lambda e: e.tensor_tensor(Lm4[:], AA4[:], lvq, ALU.mult), reads=[AA4b, cbb], writes=[Lm4b])
                            def zd(l=l):
                                if l in ACT_LEVELS:
                                    kb.op("act", lambda e: e.copy(TR4[:], pbank[:, :]), writes=[bb[pbk], TR4b])
                                elif l == 0:
                                    kb.op("dve", lambda e: e.tensor_tensor(TR4[:, :].rearrange("p (b c) -> p b c", b=4),
                                                                           nZ4[:, :].rearrange("p (b c) -> p b c", b=4),
                                                                           identb.unsqueeze(1).to_broadcast([128, 4, 128]), ALU.add),
                                          reads=[cbb, nZ4b], writes=[TR4b])
                                else:
                                    kb.op("dve", lambda e: e.tensor_tensor(TR4[:], pbank[:, :], TR4[:], ALU.add), writes=[bb[pbk], TR4b])
                        else:
                            def za(Tc=Tc, Rc=Rc, rd=rd):
                                kb.group("pe", [mm(pbank[:, 128:256], L_op[0], Rc[0]), mm(pbank[:, 256:384], LT_op[1], Tc[1])],
                                         reads=rd, writes=[bb[pbk]])
                            def zb(lv=lv):
                                kb.op("dve", lambda e: e.tensor_tensor(nZ4[:, 128:384], pbank[:, 128:384], cb[:, C_LV6X:C_LV6X + 256], ALU.mult),
                                      reads=[cbb], writes=[bb[pbk], nZ4b])
                            def zc(Tc=Tc, Rc=Rc, rd=rd):
                                kb.group("pe", [mm(pbank[:, 128:256], Tc[0], nZ4[:, 128:256]),
                                                mm(pbank[:, 256:384], Rc[1], nZ4[:, 256:384])],
                                         reads=rd + [nZ4b], writes=[bb[pbk]])
                            def zd():
                                kb.op("dve", lambda e: e.tensor_tensor(TR4[:, 128:384], pbank[:, 128:384], TR4[:, 128:384], ALU.add),
                                      writes=[bb[pbk], TR4b])
                        st += [za, zb, zc, zd]
                    n_pre = len(st)
                    Rf = (TR4[:, 128:256], TR4[:, 256:384])

                    def s1():
                        for d in range(2):
                            sbank = banks[6 + d]
                            c0 = cc[d] * 128
                            kb.group("pe", [mm(sbank[:, 0:128], rw2[:, d * 128:(d + 1) * 128], Rf[d]),
                                            mm(sbank[:, 128:256], QT[:, c0:c0 + 128], S16[d][:])],
                                     reads=[rw2b, TR4b, QTb, S16b[d]], writes=[bb[6 + d]])
                    def s2():
                        for d in range(2):
                            sbank = banks[6 + d]
                            kb.op("act", lambda e: e.activation(nwT[d][:], sbank[:, 0:128], AF.Copy, scale=-1.0), writes=[bb[6 + d], nwTb[d]])
                            kb.op("act", lambda e: e.activation(qs_[d][:], sbank[:, 128:256], AF.Copy, scale=tabs["egc"][:, it, d:d + 1]),
                                  reads=[hsb], writes=[bb[6 + d], qsb[d]])
                    def s3():
                        for d in range(2):
                            sbank = banks[6 + d]
                            kb.group("pe", [mm(sbank[:, 256:384], Rf[d], rv2[:, d * 128:(d + 1) * 128], start=True, stop=False),
                                            mm(sbank[:, 256:384], nwT[d][:], S16[d][:], start=False, stop=True)],
                                     reads=[TR4b, rv2b, nwTb[d], S16b[d]], writes=[bb[6 + d]])
                    def s4():
                        for d in range(2):
                            sbank = banks[6 + d]
                            kb.op("act", lambda e: e.copy(vn[d][:], sbank[:, 256:384]), writes=[bb[6 + d], vnb[d]])
                    def s5():
                        for d in range(2):
                            sbank = banks[6 + d]
                            kb.group("pe", [mm(sbank[:, 384:512], identb, qs_[d][:], start=True, stop=False),
                                            mm(sbank[:, 384:512], qk2[:, d * 128:(d + 1) * 128], vn[d][:], start=False, stop=True),
                                            mm(sbank[:, 0:128], kt2[:, d * 128:(d + 1) * 128], vn[d][:])],
                                     reads=[cbb, qsb[d], qk2b, vnb[d], kt2b], writes=[bb[6 + d]])
                    def s6():
                        for d in range(2):
                            sbank = banks[6 + d]
                            c = cc[d]
                            eglc = tabs["egl"][:, it, d:d + 1]
                            kb.op("dve", lambda e: e.scalar_tensor_tensor(S16[d][:], S32[d][:], eglc, sbank[:, 0:128], ALU.mult, ALU.add),
                                  reads=[hsb, S32b[d]], writes=[bb[6 + d], S16b[d]])
                            kb.op("dve", lambda e: e.scalar_tensor_tensor(S32[d][:], S32[d][:], eglc, sbank[:, 0:128], ALU.mult, ALU.add),
                                  reads=[hsb], writes=[bb[6 + d], S32b[d]])
                            if it < NCH // 2:
                                kb.op("act", lambda e: e.copy(oacc[:, c, :], sbank[:, 384:512]), writes=[bb[6 + d], oaccb[c]])
                            else:
                                kb.op("dve", lambda e: e.tensor_tensor(oacc[:, c, :], sbank[:, 384:512], oacc[:, c, :], ALU.add),
                                      writes=[bb[6 + d], oaccb[c]])
                    def s7():
                        for d in range(2):
                            kb.op("act", lambda e: e.copy(S16[d][:], S32[d][:]), reads=[S32b[d]], writes=[S16b[d]])
                    st += [s1, s2, s3, s4, s5, s6]
                    return st, n_pre

                def epilogue():
                    if dbg and h == 0:
                        dump("oacc", oacc[:], [128, NCH, 128], F32, oaccb)
                    prej = pre[:, 2:T + 2].rearrange("p (c k) -> p c k", c=NCH)
                    kb.op("dve", lambda e: e.tensor_tensor(prej, oacc[:], oacc[:], ALU.mult), reads=oaccb, writes=[preb])
                    kb.op("dve", lambda e: e.tensor_reduce(ssq[:, 0:NCH], prej, AX.X, ALU.add), reads=[preb], writes=[ssqb])
                    kb.op("act", lambda e: e.activation(ssq[:, NCH:2 * NCH], ssq[:, 0:NCH], AF.Ln, bias=epsc[:, 0:1], scale=1.0 / 128.0),
                          reads=[epsb], writes=[ssqb])
                    kb.op("act", lambda e: e.activation(ssq[:, NCH:2 * NCH], ssq[:, NCH:2 * NCH], AF.Exp, scale=-0.5), writes=[ssqb])
                    rs_bc = ssq[:, NCH:2 * NCH].unsqueeze(2).to_broadcast([128, NCH, 128])
                    kb.op("dve", lambda e: e.tensor_tensor(prej, oacc[:], rs_bc, ALU.mult), reads=oaccb + [ssqb], writes=[preb])
                    ZsT = hb5[hp][4]
                    oTs = hb5[hp][2]
                    for c8 in range(4):
                        bk = 6 + c8 % 2
                        v16 = bank16(bk, 0, 512)
                        kb.group("pe", [(lambda e, j=j: e.transpose(v16[:, j * 128:(j + 1) * 128], prej[:, c8 * 8 + j, :], identb))
                                        for j in range(8)], reads=[preb, cbb], writes=[bb[bk]])
                        kb.op("dve", lambda e: e.tensor_tensor(oTs[:, c8 * 1024:(c8 + 1) * 1024], v16, ZsT[:, c8 * 1024:(c8 + 1) * 1024], ALU.mult),
                              reads=[Zsb], writes=[bb[bk], Ktokb])
                    if dbg and h == 0:
                        dump("ogT", oTs[:, :], [128, T], BF16, [Ktokb])
                    kb.dma("pool", oT_d[h], oTs[:, :], reads=[Ktokb], writes=[oTb[h]])

                return prologue, state_init, pair_stages, epilogue

            heads = [make_head(h) for h in range(nheads)]
            stream = [(h, it) for h in range(nheads) for it in range(NCH)]
            load_head(0)
            if nheads > 1:
                load_head(1)
            active = []
            nxt = 0
            scan_done = -1
            tick = 0
            STAG = 6
            while nxt < len(stream) or active:
                if nxt < len(stream) and len(active) < NSLOT and tick % STAG == 0:
                    h, it = stream[nxt]
                    if it == 0:
                        heads[h][0](nxt % NSLOT)
                    stg_list, n_pre = heads[h][2](it, nxt % NSLOT)
                    active.append([stg_list, 0, nxt, n_pre])
                    nxt += 1
                for a_ in list(active):
                    if a_[1] >= a_[3] and a_[2] != scan_done + 1:
                        continue
                    if a_[1] == a_[3] and stream[a_[2]][1] == 0:
                        heads[stream[a_[2]][0]][1]()
                    a_[0][a_[1]]()
                    a_[1] += 1
                    if a_[1] == len(a_[0]):
                        active.remove(a_)
                        scan_done = a_[2]
                        if stream[a_[2]][1] == NCH - 1:
                            hd = stream[a_[2]][0]
                            heads[hd][3]()
                            if hd + 2 < nheads:
                                load_head(hd + 2)
                tick += 1
            kb.barrier()


    if 3 in phases:
        with contextlib.ExitStack() as st3:
            extra_stg(st3, 4, "p3")
            dgc = kb.sb("dgc", [128, 8 * CK, 128], BF16, st3)
            dgcb = [Buf("dgc%d" % i) for i in range(8)]
            for cbk in range(8):
                kb.op("pool", lambda e: e.tensor_tensor(dgc[:, cbk * CK:(cbk + 1) * CK, :], identb.unsqueeze(1).to_broadcast([128, CK, 128]),
                                                        pv[:, P_CW + cbk * CK:P_CW + (cbk + 1) * CK].unsqueeze(2).to_broadcast([128, CK, 128]),
                                                        ALU.mult), reads=[cbb, pvb], writes=[dgcb[cbk]])
            wco = kb.sb("wco", [128, 8, D], BF16, st3)
            wcob = Buf("wco")
            load_cast(wco, wcob, wco_d, 8, D)
            cwin = [kb.sb("cwin%d" % i, [128, 8, 542], BF16, st3) for i in range(2)]
            cwinb = [Buf("cwin%d" % i) for i in range(2)]
            cv = kb.sb("cv", [128, 8, 512], BF16, st3)
            cvb = Buf("cv")
            sqv = [kb.sb("sqv%d" % i, [128, 512], BF16, st3) for i in range(2)]
            sqvb = [Buf("sqv%d" % i) for i in range(2)]
            mean = kb.sb("mean", [128, 512], F32, st3)
            msq = kb.sb("msq", [128, 512], F32, st3)
            rstd = kb.sb("rstd", [128, 512], F32, st3)
            stb = Buf("lnstat")
            t1 = [kb.sb("t1_%d" % i, [128, 512], F32, st3) for i in range(2)]
            t1b = [Buf("t1_%d" % i) for i in range(2)]
            csl = kb.sb("csl", [128, 8, 512], BF16, st3)
            cslb = Buf("csl")
            ycs = [kb.sb("ycs%d" % i, [128, 8, 512], BF16, st3) for i in range(2)]
            ycsb = [Buf("ycs%d" % i) for i in range(2)]
            cv2 = [cv, kb.sb("cvB", [128, 8, 512], BF16, st3)]
            cv2b = [cvb, Buf("cvB")]

            def conv_blocks(t8, cbks):
                w = t8 % 2
                cvx, cvxb = cv2[w], cv2b[w]
                if cbks[0] == 0:
                    kb.dma("sp", cwin[w][:], cT_d[:, :, t8 * 512:t8 * 512 + 542].rearrange("c p t -> p c t"),
                           reads=[cTb], writes=[cwinb[w]])

                def stats_mm(cb_):
                    kb.op("pe", mm(banks[2][:, :], onesb, cvx[:, cb_, :], start=(cb_ == 0), stop=(cb_ == 7)),
                          reads=[cbb, cvxb], writes=[bb[2]])
                    kb.op("pe", mm(banks[3][:, :], onesb, sqv[cb_ % 2][:], start=(cb_ == 0), stop=(cb_ == 7)),
                          reads=[cbb, sqvb[cb_ % 2]], writes=[bb[3]])
                for cbk in cbks:
                    bk = cbk % 2
                    s = cbk % 2
                    kb.group("pe", [mm(banks[bk][:, :], dgc[:, cbk * CK + jj, :], cwin[w][:, cbk, jj:jj + 512],
                                       start=(jj == 0), stop=(jj == CK - 1)) for jj in range(CK)],
                             reads=[dgcb[cbk], cwinb[w]], writes=[bb[bk]])
                    bcol = pv[:, P_CB + cbk:P_CB + cbk + 1]
                    kb.op("act", lambda e: e.activation(cvx[:, cbk, :], banks[bk][:, :], AF.Identity, bias=bcol),
                          reads=[pvb], writes=[bb[bk], cvxb])
                    kb.op("act", lambda e: e.activation(sqv[s][:], banks[bk][:, :], AF.Square, bias=bcol),
                          reads=[pvb], writes=[bb[bk], sqvb[s]])
                    if cbk >= 1:
                        stats_mm(cbk - 1)
                if cbks[-1] == 7:
                    stats_mm(7)

            def ln_part(t8):
                w = t8 % 2
                cvx, cvxb = cv2[w], cv2b[w]
                kb.op("dve", lambda e: e.tensor_scalar(mean[:], banks[2][:, :], 1.0 / 1024.0, None, ALU.mult),
                      writes=[bb[2], stb])
                kb.op("dve", lambda e: e.tensor_tensor(msq[:], mean[:], mean[:], ALU.mult), writes=[stb])
                kb.op("dve", lambda e: e.scalar_tensor_tensor(msq[:], banks[3][:, :], 1.0 / 1024.0, msq[:], ALU.mult, ALU.subtract),
                      writes=[bb[3], stb])
                kb.op("act", lambda e: e.activation(rstd[:], msq[:], AF.Ln, bias=epsc[:, 2:3]), reads=[epsb], writes=[stb])
                kb.op("act", lambda e: e.activation(rstd[:], rstd[:], AF.Exp, scale=-0.5), writes=[stb])
                for cbk in range(8):
                    s = cbk % 2
                    kb.op("dve", lambda e: e.tensor_tensor(t1[s][:], cvx[:, cbk, :], mean[:], ALU.subtract),
                          reads=[cvxb, stb], writes=[t1b[s]])
                    kb.op("pool", lambda e: e.tensor_tensor(t1[s][:], t1[s][:], rstd[:], ALU.mult),
                          reads=[stb], writes=[t1b[s]])
                    kb.op("act", lambda e: e.activation(csl[:, cbk, :], t1[s][:], AF.Silu,
                                                        bias=pv[:, P_LNB + cbk:P_LNB + cbk + 1],
                                                        scale=pv[:, P_LNG + cbk:P_LNG + cbk + 1]),
                          reads=[t1b[s], pvb], writes=[cslb])

            def yconv(t8):
                w = t8 % 2
                for nb in range(8):
                    bk = 4 + (nb % 2)
                    kb.group("pe", [mm(banks[bk][:, :], wco[:, cbk, nb * 128:(nb + 1) * 128], csl[:, cbk, :],
                                       start=(cbk == 0), stop=(cbk == 7)) for cbk in range(8)],
                             reads=[wcob, cslb], writes=[bb[bk]])
                    kb.op("dve", lambda e: e.tensor_copy(ycs[w][:, nb, :], banks[bk][:, :]), writes=[bb[bk], ycsb[w]])
                kb.dma("pool", ycT_d[:, :, t8 * 512:(t8 + 1) * 512].rearrange("c p t -> p c t"), ycs[w][:],
                       reads=[ycsb[w]], writes=[ycTb])
                if dbg and t8 == 0:
                    dump("cv", cv2[w][:], [128, 8, 512], BF16, [cv2b[w]])
                    dump("ycs", ycs[w][:], [128, 8, 512], BF16, [ycsb[w]])

            conv_blocks(0, list(range(8)))
            for t8 in range(8):
                ln_part(t8)
                if t8 + 1 < 8:
                    conv_blocks(t8 + 1, [0, 1, 2])
                yconv(t8)
                if t8 + 1 < 8:
                    conv_blocks(t8 + 1, [3, 4, 5, 6, 7])
            kb.barrier()

    if 4 in phases:
        with contextlib.ExitStack() as st4:
            extra_stg(st4, 6, "p4")
            wdo = kb.sb("wdo", [128, 8, D], BF16, st4)
            wdob = Buf("wdo")
            for hh in range(8):
                for n0 in (0, 512):
                    s = stg_i[0] % len(stg)
                    stg_i[0] += 1
                    kb.dma("sp", stg[s][:], wdo_d[:, hh, n0:n0 + 512], writes=[stgb[s]])
                    kb.op("dve", lambda e: e.tensor_scalar(wdo[:, hh, n0:n0 + 512], stg[s][:], pv[:, P_DNG:P_DNG + 1], None, ALU.mult),
                          reads=[stgb[s], pvb], writes=[wdob])
            wo = kb.sb("wo", [128, 8, D], BF16, st4)
            wob = Buf("wo")
            load_cast(wo, wob, wo_d, 8, D)
            ycl = [kb.sb("ycl%d" % i, [128, 8, 512], BF16, st4) for i in range(2)]
            yclb = [Buf("ycl%d" % i) for i in range(2)]
            otl = [kb.sb("otl%d" % i, [128, 8, 512], BF16, st4) for i in range(2)]
            otlb = [Buf("otl%d" % i) for i in range(2)]
            gl = [kb.sb("gl%d" % i, [128, 16, 512], BF16, st4) for i in range(2)]
            glb = [Buf("gl%d" % i) for i in range(2)]
            m1 = [kb.sb("m1_%d" % i, [128, 512], F32, st4) for i in range(2)]
            m1b = [Buf("m1_%d" % i) for i in range(2)]
            m2 = [kb.sb("m2_%d" % i, [128, 512], F32, st4) for i in range(2)]
            m2b = [Buf("m2_%d" % i) for i in range(2)]
            yT2 = [kb.sb("yT%d" % i, [128, 8, 512], BF16, st4) for i in range(2)]
            yT2b = [Buf("yT%d" % i) for i in range(2)]
            xt = [kb.sb("xt4_%d" % i, [128, D], F32, st4) for i in range(2)]
            xtb = [Buf("xt4_%d" % i) for i in range(2)]

            def wo_unit(t8, unit):
                yT, yTb = yT2[t8 % 2], yT2b[t8 % 2]
                sub, half = unit // 2, unit % 2
                r0 = t8 * 512 + sub * 128
                xs_ = (t8 * 4 + sub) % 2
                if half == 0:
                    kb.dma("sp", xt[xs_][:], x_d[r0:r0 + 128, :], writes=[xtb[xs_]])
                bk = 6 + half
                kb.group("pe", [mm(banks[bk][:, :], yT[:, kbk, sub * 128:(sub + 1) * 128], wo[:, kbk, half * 512:(half + 1) * 512],
                                   start=(kbk == 0), stop=(kbk == 7)) for kbk in range(8)],
                         reads=[yTb, wob], writes=[bb[bk]])
                kb.op("dve", lambda e: e.tensor_tensor(xt[xs_][:, half * 512:(half + 1) * 512], banks[bk][:, :],
                                                       xt[xs_][:, half * 512:(half + 1) * 512], ALU.add),
                      writes=[bb[bk], xtb[xs_]])
                if half == 1:
                    kb.dma("pool", x1_d[r0:r0 + 128, :], xt[xs_][:], reads=[xtb[xs_]], writes=[x1b])

            for t8 in range(8):
                w = t8 % 2
                yT, yTb = yT2[w], yT2b[w]
                tsl = slice(t8 * 512, (t8 + 1) * 512)
                kb.dma("sp", ycl[w][:], ycT_d[:, :, tsl].rearrange("c p t -> p c t"), reads=[ycTb], writes=[yclb[w]])
                kb.dma("sp", otl[w][:], oT_d[:, :, tsl].rearrange("c p t -> p c t"), reads=oTb, writes=[otlb[w]])
                kb.dma("sp", gl[w][:], gT_d[:, :, tsl].rearrange("c p t -> p c t"), reads=[gTb], writes=[glb[w]])
                for nb in range(8):
                    s = nb % 2
                    b2 = nb % 4
                    kb.group("pe", [mm(banks[b2][:, :], wdo[:, hh, nb * 128:(nb + 1) * 128], otl[w][:, hh, :],
                                       start=(hh == 0), stop=(hh == 7)) for hh in range(8)],
                             reads=[wdob, otlb[w]], writes=[bb[b2]])
                    kb.op("dve", lambda e: e.tensor_tensor(m1[s][:], gl[w][:, nb, :], ycl[w][:, nb, :], ALU.mult),
                          reads=[glb[w], yclb[w]], writes=[m1b[s]])
                    kb.op("dve", lambda e: e.tensor_tensor(m2[s][:], banks[b2][:, :], gl[w][:, 8 + nb, :], ALU.mult),
                          reads=[glb[w]], writes=[bb[b2], m2b[s]])
                    kb.op("pool", lambda e: e.tensor_tensor(yT[:, nb, :], m1[s][:], m2[s][:], ALU.add),
                          reads=[m1b[s], m2b[s]], writes=[yTb])
                    if t8 > 0:
                        wo_unit(t8 - 1, nb)
                if dbg and t8 == 0:
                    dump("yT", yT[:], [128, 8, 512], BF16, [yTb])
            for unit in range(8):
                wo_unit(7, unit)
            kb.barrier()

    if 5 in phases:
        with contextlib.ExitStack() as st5:
            extra_stg(st5, 4, "p5")
            wgu = kb.sb("wgu", [128, 8, 2 * DFF], BF16, st5)
            wgub = [Buf("wgu%d" % i) for i in range(11)]
            wgu_order = [5, 0, 6, 1, 7, 2, 8, 3, 9, 4, 10]
            wgu_done = []

            def wgu_need(n):
                while len(wgu_done) < min(n, len(wgu_order)):
                    ci = wgu_order[len(wgu_done)]
                    load_cast(wgu, wgub, wgu_d, 8, 2 * DFF, scale_col=P_GFFN, order=[ci])
                    wgu_done.append(ci)
            wgu_need(3)
            wdn = kb.sb("wdn", [128, NFB, D], BF16, st5)
            wdnb = [Buf("wdn%d" % i) for i in range(2)]
            wdn_loaded = [False]
            x1t = [kb.sb("x1t%d" % i, [128, 2, D], F32, st5) for i in range(2)]
            x1tb = [Buf("x1t%d" % i) for i in range(2)]
            h2T = [kb.sb("h2T%d" % i, [128, 8, 256], BF16, st5) for i in range(2)]
            h2Tb = [Buf("h2T%d" % i) for i in range(2)]
            aT = kb.sb("aT", [128, NFB, 256], BF16, st5)
            aTb = Buf("aT")
            sgt = [kb.sb("sgt%d" % i, [128, 256], F32, st5) for i in range(2)]
            sgtb = [Buf("sgt%d" % i) for i in range(2)]
            xn5 = [kb.sb("xn5_%d" % i, [128, D], BF16, st5) for i in range(2)]
            xn5b = [Buf("xn5_%d" % i) for i in range(2)]
            junk5 = [kb.sb("junk5_%d" % i, [128, D], BF16, st5) for i in range(2)]
            junk5b = [Buf("junk5_%d" % i) for i in range(2)]
            stat5 = [kb.sb("stat5_%d" % i, [128, 3 * NCH], F32, st5) for i in range(2)]
            stat5b = [Buf("stat5_%d" % i) for i in range(2)]
            for i in range(2):
                kb.op("pool", lambda e: e.memset(stat5[i][:], 0.0), writes=[stat5b[i]])
            def p5_load(t16):
                w = t16 % 2
                r0 = t16 * 256
                kb.dma("sp", x1t[w][:], x1_d[r0:r0 + 256, :].rearrange("(s p) d -> p s d", p=128), reads=[x1b], writes=[x1tb[w]])

            def p5_rms(t16, part):
                w = t16 % 2
                for sub in range(2):
                    it = t16 * 2 + sub
                    rms_to_featmajor(x1t[w][:, sub, :], x1tb[w], h2T[w], h2Tb[w], sub * 128,
                                     (junk5[sub], junk5b[sub], stat5[sub], stat5b[sub], xn5[sub], xn5b[sub], 6 + sub), it, part=part)

            def p5_gate_up(t16):
                w = t16 % 2
                for fb in range(NFB):
                    bk = fb % 4
                    s = fb % 2
                    if t16 == 0:
                        wgu_need(3 + (fb + 1) // 2)
                    kb.group("pe", [mm(banks[bk][:, 0:256], wgu[:, dc, fb * 128:(fb + 1) * 128], h2T[w][:, dc, :],
                                       start=(dc == 0), stop=(dc == 7)) for dc in range(8)] +
                                   [mm(banks[bk][:, 256:512], wgu[:, dc, DFF + fb * 128:DFF + (fb + 1) * 128], h2T[w][:, dc, :],
                                       start=(dc == 0), stop=(dc == 7)) for dc in range(8)],
                             reads=[wgub[(fb * 128) // 512], wgub[(DFF + fb * 128) // 512], h2Tb[w]], writes=[bb[bk]])
                    kb.op("act", lambda e: e.activation(sgt[s][:], banks[bk][:, 0:256], AF.Silu), writes=[bb[bk], sgtb[s]])
                    kb.op("dve", lambda e: e.tensor_tensor(aT[:, fb, :], banks[bk][:, 256:512], sgt[s][:], ALU.mult),
                          reads=[sgtb[s]], writes=[bb[bk], aTb])

            def p5_down(t16):
                w = t16 % 2
                r0 = t16 * 256
                if not wdn_loaded[0]:
                    wgu_need(11)
                    load_cast(wdn, wdnb, wdn_d, NFB, D)
                    wdn_loaded[0] = True
                for sub in range(2):
                    for half in range(2):
                        bk = 4 + half
                        kb.group("pe", [mm(banks[bk][:, :], aT[:, fb, sub * 128:(sub + 1) * 128], wdn[:, fb, half * 512:(half + 1) * 512],
                                           start=(fb == 0), stop=(fb == NFB - 1)) for fb in range(NFB)],
                                 reads=[aTb, wdnb[half]], writes=[bb[bk]])
                        kb.op("dve", lambda e: e.tensor_tensor(x1t[w][:, sub, half * 512:(half + 1) * 512], banks[bk][:, :],
                                                               x1t[w][:, sub, half * 512:(half + 1) * 512], ALU.add),
                              writes=[bb[bk], x1tb[w]])
                kb.dma("pool", x2_d[r0:r0 + 256, :].rearrange("(s p) d -> p s d", p=128), x1t[w][:], reads=[x1tb[w]], writes=[x2b])

            p5_load(0)
            p5_load(1)
            p5_rms(0, None)
            for t16 in range(16):
                p5_gate_up(t16)
                if t16 + 1 < 16:
                    p5_rms(t16 + 1, "a")
                p5_down(t16)
                if t16 + 2 < 16:
                    p5_load(t16 + 2)
                if t16 + 1 < 16:
                    p5_rms(t16 + 1, "b")
            kb.barrier()

    if 6 in phases:
        with contextlib.ExitStack() as st6:
            extra_stg(st6, 6, "p6")
            wplg = kb.sb("wplg", [128, 8, D], BF16, st6)
            wplgb = Buf("wplg")
            load_cast(wplg, wplgb, wplg_d, 8, D, scale_col=P_GPL)
            wplp = kb.sb("wplp", [128, 2, D], BF16, st6)
            wplpb = Buf("wplp")
            load_cast(wplp, wplpb, wplp_d, 2, D)
            rv6 = kb.sb("rv6", [128, 2048], F32, st6)
            rv6b = Buf("rv6")
            kb.dma("sp", rv6[:], rvec_d[:, R_GPLP:R_GPLP + 2048].partition_broadcast(128), writes=[rv6b])
            gplp_bc = rv6[:, 0:1024]
            gfin_bc = rv6[:, 1024:2048]
            NS6 = 8
            def mk6(nm, shape, dt):
                return [kb.sb("%s%d" % (nm, i), shape, dt, st6) for i in range(NS6)], [Buf("%s%d" % (nm, i)) for i in range(NS6)]
            x2t, x2tb = mk6("x2t", [128, D], F32)
            pt, ptb = mk6("pt", [128, PLE], F32)
            pbf, pbfb = mk6("pbf", [128, PLE], BF16)
            pT, pTb = mk6("pT", [128, 2, 128], BF16)
            h3T, h3Tb = mk6("h3T", [128, 8, 128], BF16)
            tg6, tg6b = mk6("tg6", [128, D], F32)
            pl6, pl6b = mk6("pl6", [128, D], F32)
            xn6, xn6b = mk6("xn6", [128, D], BF16)
            junk6 = kb.sb("junk6", [128, D], BF16, st6)
            junk6b = Buf("junk6")
            sts = [kb.sb("sts6_%d" % i, [128, 24], F32, st6) for i in range(2)]
            stsb = [[Buf("sts6_%d_%d" % (i, k)) for k in range(3)] for i in range(2)]
            def ctx6(grp):
                gp = grp % 2
                tl = [(grp * 4 + i, (grp * 4 + i) % NS6, i) for i in range(4)]
                return sts[gp], stsb[gp], tl

            def front6(grp, inter):
                stt, (sbx, sbp, sbf), tl = ctx6(grp)
                kb.op("pool", lambda e: e.memset(stt[:, 0:8], 0.0), writes=[sbx])
                kb.op("pool", lambda e: e.memset(stt[:, 8:16], 0.0), writes=[sbp])
                kb.op("pool", lambda e: e.memset(stt[:, 16:24], 0.0), writes=[sbf])
                for tt, w, i in tl:
                    r0 = tt * 128
                    kb.dma("sp", x2t[w][:], x2_d[r0:r0 + 128, :], reads=[x2b], writes=[x2tb[w]])
                    kb.dma("sp", pt[w][:], p_d[r0:r0 + 128, :], writes=[ptb[w]])
                for tt, w, i in tl:
                    kb.op("act", lambda e: e.activation(junk6[:], x2t[w][:], AF.Square, scale=1.0 / 32.0, accum_out=stt[:, i:i + 1]),
                          reads=[x2tb[w]], writes=[junk6b, sbx])
                kb.op("act", lambda e: e.activation(stt[:, 4:8], stt[:, 0:4], AF.Ln, bias=epsc[:, 0:1]), reads=[epsb], writes=[sbx])
                kb.op("act", lambda e: e.activation(stt[:, 4:8], stt[:, 4:8], AF.Exp, scale=-0.5), writes=[sbx])
                for tt, w, i in tl:
                    kb.op("dve", lambda e: e.tensor_scalar(xn6[w][:], x2t[w][:], stt[:, 4 + i:5 + i], None, ALU.mult),
                          reads=[x2tb[w], sbx], writes=[xn6b[w]])
                    kb.op("pool", lambda e: e.tensor_copy(pbf[w][:], pt[w][:]), reads=[ptb[w]], writes=[pbfb[w]])
                for tt, w, i in tl:
                    bk = 4 + i
                    pv16 = bank16(bk, 0, 512)
                    kb.group("pe", [(lambda e, c=c: e.transpose(pv16[:, c * 128:(c + 1) * 128], xn6[w][:, c * 128:(c + 1) * 128], identb))
                                    for c in range(8)], reads=[xn6b[w], cbb], writes=[bb[bk]])
                for tt, w, i in tl:
                    bk = 4 + i
                    pv16 = bank16(bk, 0, 512)
                    if i % 2 == 0:
                        kb.op("dve", lambda e: e.tensor_copy(h3T[w][:], pv16.rearrange("p (c t) -> p c t", c=8)), writes=[bb[bk], h3Tb[w]])
                    else:
                        kb.op("act", lambda e: e.copy(h3T[w][:], pv16.rearrange("p (c t) -> p c t", c=8)), writes=[bb[bk], h3Tb[w]])
                for tt, w, i in tl:
                    bk = 4 + i
                    p16 = bank16(bk, 0, 128)
                    kb.group("pe", [(lambda e, c=c: e.transpose(p16[:, c * 128:(c + 1) * 128], pbf[w][:, c * 128:(c + 1) * 128], identb))
                                    for c in range(2)], reads=[pbfb[w], cbb], writes=[bb[bk]])
                for tt, w, i in tl:
                    bk = 4 + i
                    p16 = bank16(bk, 0, 128)
                    kb.op("dve", lambda e: e.tensor_copy(pT[w][:], p16.rearrange("p (c t) -> p c t", c=2)), writes=[bb[bk], pTb[w]])
                u = 0
                for tt, w, i in tl:
                    for half in range(2):
                        hs = slice(half * 512, (half + 1) * 512)
                        bg, bp = 2 * (u % 2), 2 * (u % 2) + 1
                        kb.group("pe", [mm(banks[bg][:, :], h3T[w][:, dc, :], wplg[:, dc, hs], start=(dc == 0), stop=(dc == 7))
                                        for dc in range(8)], reads=[h3Tb[w], wplgb], writes=[bb[bg]])
                        kb.group("pe", [mm(banks[bp][:, :], pT[w][:, kc, :], wplp[:, kc, hs], start=(kc == 0), stop=(kc == 1))
                                        for kc in range(2)], reads=[pTb[w], wplpb], writes=[bb[bp]])
                        kb.op("act", lambda e: e.activation(tg6[w][:, hs], banks[bg][:, :], AF.Tanh, scale=0.5), writes=[bb[bg], tg6b[w]])
                        kb.op("act", lambda e: e.copy(pl6[w][:, hs], banks[bp][:, :]), writes=[bb[bp], pl6b[w]])
                        if inter[u] is not None:
                            inter[u]()
                        u += 1
                    kb.op("act", lambda e: e.activation(junk6[:], pl6[w][:], AF.Square, scale=1.0 / 32.0, accum_out=stt[:, 8 + i:9 + i]),
                          reads=[pl6b[w]], writes=[junk6b, sbp])

            def back6_parts(grp):
                stt, (sbx, sbp, sbf), tl = ctx6(grp)

                def E():
                    kb.op("act", lambda e: e.activation(stt[:, 12:16], stt[:, 8:12], AF.Ln, bias=epsc[:, 0:1]), reads=[epsb], writes=[sbp])
                    kb.op("act", lambda e: e.activation(stt[:, 12:16], stt[:, 12:16], AF.Exp, scale=-0.5), writes=[sbp])
                    kb.op("dve", lambda e: e.tensor_scalar(stt[:, 12:16], stt[:, 12:16], 0.5, None, ALU.mult), writes=[sbp])

                def F(k):
                    tt, w, i = tl[k]
                    kb.op("pool", lambda e: e.tensor_tensor(pl6[w][:], pl6[w][:], gplp_bc, ALU.mult), reads=[rv6b], writes=[pl6b[w]])
                    kb.op("dve", lambda e: e.scalar_tensor_tensor(tg6[w][:], tg6[w][:], 1.0, pl6[w][:], ALU.add, ALU.mult),
                          reads=[pl6b[w]], writes=[tg6b[w]])
                    kb.op("dve", lambda e: e.scalar_tensor_tensor(x2t[w][:], tg6[w][:], stt[:, 12 + i:13 + i], x2t[w][:], ALU.mult, ALU.add),
                          reads=[tg6b[w], sbp], writes=[x2tb[w]])
                    kb.op("act", lambda e: e.activation(junk6[:], x2t[w][:], AF.Square, scale=1.0 / 32.0, accum_out=stt[:, 16 + i:17 + i]),
                          reads=[x2tb[w]], writes=[junk6b, sbf])

                def Gs():
                    kb.op("act", lambda e: e.activation(stt[:, 20:24], stt[:, 16:20], AF.Ln, bias=epsc[:, 0:1]), reads=[epsb], writes=[sbf])
                    kb.op("act", lambda e: e.activation(stt[:, 20:24], stt[:, 20:24], AF.Exp, scale=-0.5), writes=[sbf])

                def H(k):
                    tt, w, i = tl[k]
                    r0 = tt * 128
                    kb.op("dve", lambda e: e.scalar_tensor_tensor(tg6[w][:], x2t[w][:], stt[:, 20 + i:21 + i], gfin_bc, ALU.mult, ALU.mult),
                          reads=[x2tb[w], sbf, rv6b], writes=[tg6b[w]])
                    kb.dma("pool", out_d[r0:r0 + 128, :], tg6[w][:], reads=[tg6b[w]], writes=[outb])
                return E, F, Gs, H

            NG6 = NCH // 4
            for grp in range(NG6):
                if grp == 0:
                    inter = [None] * 8
                else:
                    E, F, Gs, H = back6_parts(grp - 1)
                    E()
                    def mkF(k, F=F, Gs=Gs):
                        def f():
                            F(k)
                            if k == 3:
                                Gs()
                        return f
                    inter = [mkF(k) for k in range(4)] + [(lambda k=k, H=H: H(k)) for k in range(4)]
                front6(grp, inter)
            E, F, Gs, H = back6_parts(NG6 - 1)
            E()
            for k in range(4):
                F(k)
            Gs()
            for k in range(4):
                H(k)
            kb.barrier()

    for b in list(dbg_outs.values()) + [outb]:
        kb._wait("sp", b.writer)
    return nc, kb


def host_prep(inputs):
    f = np.float32
    w_in = np.asarray(inputs["w_in"][0], f)

    def pk(w, kc):
        return np.ascontiguousarray(w.reshape(kc, 128, w.shape[1]).transpose(1, 0, 2))

    w_heads = np.empty((NH, 128, 8, 512), f)
    for h in range(NH):
        for j, off in enumerate((O_Q, O_K, O_V, O_Z)):
            w_heads[h, :, :, j * 128:(j + 1) * 128] = pk(w_in[:, off + h * 128:off + (h + 1) * 128], 8)
    shared = {
        "w_heads": w_heads,
        "w_ab": pk(w_in[:, O_A:O_A + 32], 8),
        "w_glu": pk(w_in[:, 0:2048], 8),
        "w_gate": pk(w_in[:, O_GA:O_GA + 2048], 8),
        "w_conv_out": pk(np.asarray(inputs["w_conv_out"][0], f), 8),
        "w_delta_out": pk(np.asarray(inputs["w_delta_out"][0], f), 8),
        "w_o": pk(np.asarray(inputs["w_o"][0], f), 8),
        "w_gate_up": pk(np.asarray(inputs["w_gate_up"][0], f), 8),
        "w_down": pk(np.asarray(inputs["w_down"][0], f), NFB),
        "w_pl_gate": pk(np.asarray(inputs["w_pl_gate"][0], f), 8),
        "w_pl_proj": pk(np.asarray(inputs["w_pl_proj"][0], f), 2),
    }
    cs = np.zeros((128, NCONST), f)
    i = np.arange(128)
    P, Fr = i[:, None], i[None, :]
    cs[:, C_ID:C_ID + 128] = (P == Fr)
    cs[:, C_TL:C_TL + 128] = (P <= Fr)
    cs[:, C_TG:C_TG + 128] = (P >= Fr)
    cs[:, C_ONE:C_ONE + 128] = 1.0
    sgf = np.where(P > Fr, -1.0, np.where(P < Fr, 1.0, 0.0))
    cs[:, C_SGF:C_SGF + 128] = sgf
    cs[:, C_SGB:C_SGB + 128] = -sgf
    cs[:, C_NEG:C_NEG + 128] = -1.0
    cs[:, C_MUP:C_MUP + 128] = (P <= Fr)
    cs[:, C_MLO:C_MLO + 128] = (P >= Fr)
    for l in range(7):
        b = 1 << l
        m = ((P // (2 * b)) == (Fr // (2 * b))) & ((P % (2 * b)) >= b) & ((Fr % (2 * b)) < b)
        cs[:, C_LV + l * 256:C_LV + l * 256 + 128] = -m.astype(f)
        cs[:, C_LV + l * 256 + 128:C_LV + (l + 1) * 256] = -m.T.astype(f)
        if l == 6:
            cs[:, C_LV6X:C_LV6X + 128] = -m.T.astype(f)
            cs[:, C_LV6X + 128:C_LV6X + 256] = -m.astype(f)
        if l in POOL_LV:
            o_ = C_LVQ + POOL_LV.index(l) * 512
            for bi, mk_ in enumerate((m, m.T, m.T, m)):
                cs[:, o_ + bi * 128:o_ + (bi + 1) * 128] = -mk_.astype(f)
    shared["consts"] = cs

    def pcol(v):
        return np.asarray(v, f).reshape(8, 128).T

    pvec = np.zeros((128, NP), f)
    pvec[:, P_GMIX:P_GMIX + 8] = pcol(inputs["g_mix"][0])
    pvec[:, P_GFFN:P_GFFN + 8] = pcol(inputs["g_ffn"][0])
    pvec[:, P_GPL:P_GPL + 8] = pcol(inputs["g_pl"][0])
    pvec[:, P_CB:P_CB + 8] = pcol(inputs["conv_dw_b"][0])
    pvec[:, P_LNG:P_LNG + 8] = pcol(inputs["conv_ln_g"][0])
    pvec[:, P_LNB:P_LNB + 8] = pcol(inputs["conv_ln_b"][0])
    qw = np.asarray(inputs["qkv_conv_w"][0], f)
    pvec[:, P_QKVW:P_QKVW + 120] = qw.reshape(5, 24, 128).transpose(2, 1, 0).reshape(128, 120)
    cw_ = np.asarray(inputs["conv_dw_w"][0], f)
    pvec[:, P_CW:P_CW + 8 * CK] = cw_.reshape(CK, 8, 128).transpose(2, 1, 0).reshape(128, 8 * CK)
    pvec[:, P_DNG] = np.asarray(inputs["delta_norm_g"][0], f)
    shared["pvec"] = pvec
    rvec = np.zeros((1, NR), f)
    rvec[0, R_DTB:R_DTB + 16] = np.asarray(inputs["dt_bias"][0], f).reshape(16)
    rvec[0, R_ALOG:R_ALOG + 16] = np.asarray(inputs["a_log"][0], f).reshape(16)
    rvec[0, R_DNG:R_DNG + 128] = np.asarray(inputs["delta_norm_g"][0], f)
    rvec[0, R_GPLP:R_GPLP + 1024] = np.asarray(inputs["g_pl_proj"][0], f)
    rvec[0, R_GFIN:R_GFIN + 1024] = np.asarray(inputs["g_final"], f)
    shared["rvec"] = rvec
    return shared


def kernel(**inputs):
    shared = host_prep(inputs)
    x = np.asarray(inputs["x"], np.float32)
    p = np.asarray(inputs["p"], np.float32)[0]
    nc, kb = build()
    in_maps = []
    for c in range(8):
        m = dict(shared)
        m["x"] = np.ascontiguousarray(x[c])
        m["p"] = np.ascontiguousarray(p[c])
        in_maps.append(m)
    res = run_bass_kernel_spmd(nc, in_maps, core_ids=list(range(8)))
    return np.stack([np.asarray(r["out"], np.float32) for r in res.results], axis=0)
```

```python
import contextlib
import numpy as np
import concourse.bass as bass
import concourse.mybir as mybir
from concourse.bass_utils import run_bass_kernel_spmd

F32 = mybir.dt.float32
BF16 = mybir.dt.bfloat16
AF = mybir.ActivationFunctionType
ALU = mybir.AluOpType
AX = mybir.AxisListType

T = 4096
D = 1024
NH = 8
NCH = 32
DFF = 2816
NFB = 22
PLE = 256
CK = 31
EPS = 1e-6
LN_EPS = 1e-5

O_GLUA, O_GLUB, O_Q, O_K, O_V, O_Z, O_A, O_B, O_GA, O_GB = 0, 1024, 2048, 3072, 4096, 5120, 6144, 6160, 6176, 7200

C_ID, C_TL, C_TG, C_ONE, C_SGF, C_SGB, C_NEG, C_MUP, C_MLO, C_LV = 0, 128, 256, 384, 512, 640, 768, 896, 1024, 1152
C_LV6X = C_LV + 7 * 256
C_LVQ = C_LV6X + 256
POOL_LV = (3,)
NCONST = C_LVQ + len(POOL_LV) * 512
P_GMIX, P_GFFN, P_GPL, P_CB, P_LNG, P_LNB, P_QKVW, P_CW = 0, 8, 16, 24, 32, 40, 48, 168
P_DNG = P_CW + 8 * CK
NP = P_DNG + 1
R_DTB, R_ALOG, R_DNG, R_GPLP, R_GFIN = 0, 16, 32, 160, 1184
NR = R_GFIN + 1024


class Buf:
    __slots__ = ("name", "writer", "readers", "dsem", "dcnt")

    def __init__(self, name):
        self.name = name
        self.writer = None
        self.readers = {}
        self.dsem = None
        self.dcnt = 0


class KB:
    def __init__(self):
        self.nc = bass.Bass("TRN2", target_bir_lowering=False)
        nc = self.nc
        self.engs = {"pe": nc.tensor, "act": nc.scalar, "dve": nc.vector,
                     "pool": nc.gpsimd, "sp": nc.sync}
        self.sems = {}
        self.cnt = {}
        for e in ("pe", "act", "dve", "pool"):
            self.sems[e] = nc.alloc_semaphore("s_" + e)
            self.cnt[e] = 0
        self.waited = {}
        self.dbufs = []
        self.nops = 0
        self.root = contextlib.ExitStack()

    def sb(self, name, shape, dt, st=None):
        return (st or self.root).enter_context(self.nc.sbuf_tensor(name, list(shape), dt))

    def _wait(self, eng, tok):
        if tok is None:
            return
        key, val = tok
        if val <= self.waited.get((eng, key), 0):
            return
        self.engs[eng].wait_ge(self.sems[key], val)
        self.waited[(eng, key)] = val

    def _deps(self, eng, reads, writes):
        for b in reads:
            self._wait(eng, b.writer)
        for b in writes:
            self._wait(eng, b.writer)
            for k, v in list(b.readers.items()):
                self._wait(eng, (k, v))

    def _commit(self, tok, reads, writes):
        k, v = tok
        for b in reads:
            if b.readers.get(k, 0) < v:
                b.readers[k] = v
        for b in writes:
            b.writer = tok
            b.readers = {}

    def op(self, eng, fn, reads=(), writes=()):
        self._deps(eng, reads, writes)
        ins = fn(self.engs[eng])
        self.cnt[eng] += 1
        ins.then_inc(self.sems[eng], 1)
        tok = (eng, self.cnt[eng])
        self._commit(tok, reads, writes)
        self.nops += 1
        return tok

    def group(self, eng, fns, reads=(), writes=()):
        self._deps(eng, reads, writes)
        ins = None
        for fn in fns:
            ins = fn(self.engs[eng])
            self.nops += 1
        self.cnt[eng] += 1
        ins.then_inc(self.sems[eng], 1)
        tok = (eng, self.cnt[eng])
        self._commit(tok, reads, writes)
        return tok

    def dma(self, q, out, in_, reads=(), writes=(), **kw):
        self._deps(q, reads, writes)
        b = writes[0]
        if b.dsem is None:
            b.dsem = "d%d" % len(self.dbufs)
            self.dbufs.append(b)
            self.sems[b.dsem] = self.nc.alloc_semaphore(b.dsem)
        ins = self.engs[q].dma_start(out=out, in_=in_, **kw)
        ins.then_inc(self.sems[b.dsem], 16)
        b.dcnt += 16
        tok = (b.dsem, b.dcnt)
        self._commit(tok, reads, writes)
        self.nops += 1
        return tok

    def barrier(self):
        for e in ("pe", "act", "dve", "pool", "sp"):
            for k in ("pe", "act", "dve", "pool"):
                if self.cnt[k] > 0:
                    self._wait(e, (k, self.cnt[k]))
            for b in self.dbufs:
                if b.dcnt > 0:
                    self._wait(e, (b.dsem, b.dcnt))


def mm(out, lhsT, rhs, start=True, stop=True):
    return lambda e: e.matmul(out, lhsT, rhs, start=start, stop=stop)


def build(dbg=False, phases=(0, 1, 2, 3, 4, 5, 6), nheads=NH):
    kb = KB()
    nc = kb.nc

    def din(name, shape, dt=F32):
        return nc.dram_tensor(name, list(shape), dt, kind="ExternalInput").ap()

    x_d = din("x", [T, D])
    p_d = din("p", [T, PLE])
    wh_d = din("w_heads", [NH, 128, 8, 512])
    wab_d = din("w_ab", [128, 8, 32])
    wglu_d = din("w_glu", [128, 8, 2048])
    wgate_d = din("w_gate", [128, 8, 2048])
    wco_d = din("w_conv_out", [128, 8, D])
    wdo_d = din("w_delta_out", [128, 8, D])
    wo_d = din("w_o", [128, 8, D])
    wgu_d = din("w_gate_up", [128, 8, 2 * DFF])
    wdn_d = din("w_down", [128, NFB, D])
    wplg_d = din("w_pl_gate", [128, 8, D])
    wplp_d = din("w_pl_proj", [128, 2, D])
    consts_d = din("consts", [128, NCONST])
    pvec_d = din("pvec", [128, NP])
    rvec_d = din("rvec", [1, NR])
    out_d = nc.dram_tensor("out", [T, D], F32, kind="ExternalOutput").ap()
    oT_d = nc.dram_tensor("oT_s", [NH, 128, T], BF16, kind="Internal").ap()
    cT_d = nc.dram_tensor("cT_s", [8, 128, T + 30], BF16, kind="Internal").ap()
    ycT_d = nc.dram_tensor("ycT_s", [8, 128, T], BF16, kind="Internal").ap()
    x1_d = nc.dram_tensor("x1_s", [T, D], F32, kind="Internal").ap()
    gT_d = nc.dram_tensor("gT_s", [16, 128, T], BF16, kind="Internal").ap()
    x2_d = nc.dram_tensor("x2_s", [T, D], F32, kind="Internal").ap()
    hs_d = nc.dram_tensor("hs_s", [NH, 5, 128, T], BF16, kind="Internal").ap()
    hsdb = [Buf("hs_d%d" % i) for i in range(NH)]
    oTb = [Buf("oT_d%d" % i) for i in range(NH)]
    cTb = Buf("cT_d")
    ycTb = Buf("ycT_d")
    x1b = Buf("x1_d")
    x2b = Buf("x2_d")
    gTb = Buf("gT_d")
    outb = Buf("out_d")

    dbg_outs = {}

    def dump(name, ap, shape, dt, rb):
        if not dbg:
            return
        o = nc.dram_tensor("dbg_" + name, list(shape), dt, kind="ExternalOutput").ap()
        b = Buf("dbg_" + name)
        kb.dma("sp", o, ap, reads=rb, writes=[b])
        dbg_outs[name] = b

    banks = [nc.alloc_psum_tensor("bank%d" % i, [128, 512], F32) for i in range(8)]
    bb = [Buf("bank%d" % i) for i in range(8)]

    def bank16(i, lo, hi):
        return banks[i][:, lo:hi].bitcast(BF16)

    cf = kb.sb("cf", [128, C_MUP], F32)
    cb = kb.sb("cb", [128, NCONST], BF16)
    pv = kb.sb("pv", [128, NP], F32)
    epsc = kb.sb("epsc", [128, 4], F32)
    cfb, cbb, pvb, epsb = Buf("cf"), Buf("cb"), Buf("pv"), Buf("eps")
    NSTG = 2
    stg = [kb.sb("stg%d" % i, [128, 512], F32) for i in range(NSTG)]
    stgb = [Buf("stg%d" % i) for i in range(NSTG)]
    stg_i = [0]

    kb.dma("sp", cf[:], consts_d[:, 0:C_MUP], writes=[cfb])
    kb.dma("sp", pv[:], pvec_d, writes=[pvb])
    kb.op("pool", lambda e: e.memset(epsc[:, 0:1], EPS), writes=[epsb])
    kb.op("pool", lambda e: e.memset(epsc[:, 1:2], 128.0 * EPS), writes=[epsb])
    kb.op("pool", lambda e: e.memset(epsc[:, 2:3], LN_EPS), writes=[epsb])
    kb.op("pool", lambda e: e.memset(epsc[:, 3:4], -0.5), writes=[epsb])
    for n0 in range(0, NCONST, 512):
        n1 = min(NCONST, n0 + 512)
        s = stg_i[0] % len(stg)
        stg_i[0] += 1
        kb.dma("sp", stg[s][:, 0:n1 - n0], consts_d[:, n0:n1], writes=[stgb[s]])
        kb.op("pool", lambda e: e.tensor_copy(cb[:, n0:n1], stg[s][:, 0:n1 - n0]), reads=[stgb[s]], writes=[cbb])
    identb = cb[:, C_ID:C_ID + 128]
    onesb = cb[:, C_ONE:C_ONE + 128]

    lc_i = [0]

    def drop_extra_stg():
        del stg[NSTG:]
        del stgb[NSTG:]

    def extra_stg(st, n, tag):
        drop_extra_stg()
        for i in range(n):
            stg.append(kb.sb("stgx_%s%d" % (tag, i), [128, 512], F32, st))
            stgb.append(Buf("stgx_%s%d" % (tag, i)))


    def load_cast(dst, dstb, src, KC, N, scale_col=None, scale_const=None, eng=None, order=None):
        nchunk = (N + 511) // 512
        for ci in (order if order is not None else range(nchunk)):
            n0 = ci * 512
            n1 = min(N, n0 + 512)
            db = dstb[ci] if isinstance(dstb, list) else dstb
            for kc in range(KC):
                s = stg_i[0] % len(stg)
                stg_i[0] += 1
                kb.dma("sp", stg[s][:, 0:n1 - n0], src[:, kc, n0:n1], writes=[stgb[s]])
                o = dst[:, kc, n0:n1]
                i_ = stg[s][:, 0:n1 - n0]
                if eng is None:
                    eng_ = ("act", "dve")[lc_i[0] % 2]
                    lc_i[0] += 1
                else:
                    eng_ = eng
                if eng_ == "act":
                    assert not (scale_col is not None and scale_const is not None)
                    if scale_col is not None:
                        sc = pv[:, scale_col + kc:scale_col + kc + 1]
                        kb.op("act", lambda e: e.activation(o, i_, AF.Copy, scale=sc), reads=[stgb[s], pvb], writes=[db])
                    elif scale_const is not None:
                        kb.op("act", lambda e: e.activation(o, i_, AF.Copy, scale=float(scale_const)), reads=[stgb[s]], writes=[db])
                    else:
                        kb.op("act", lambda e: e.copy(o, i_), reads=[stgb[s]], writes=[db])
                    continue
                if scale_col is not None:
                    sc = pv[:, scale_col + kc:scale_col + kc + 1]
                    if scale_const is not None:
                        kb.op(eng_, lambda e: e.tensor_scalar(o, i_, sc, float(scale_const), ALU.mult, ALU.mult),
                              reads=[stgb[s], pvb], writes=[db])
                    else:
                        kb.op(eng_, lambda e: e.tensor_scalar(o, i_, sc, None, ALU.mult),
                              reads=[stgb[s], pvb], writes=[db])
                elif scale_const is not None:
                    kb.op(eng_, lambda e: e.tensor_scalar(o, i_, float(scale_const), None, ALU.mult),
                          reads=[stgb[s]], writes=[db])
                else:
                    kb.op(eng_, lambda e: e.tensor_copy(o, i_), reads=[stgb[s]], writes=[db])

    gall = kb.sb("gall", [128, NCH, 16], F32)
    ball = kb.sb("ball", [128, NCH, 16], F32)

    stA = contextlib.ExitStack()
    hT = kb.sb("hT", [128, 8, T], BF16, stA)
    hTb = [Buf("hT%d" % i) for i in range(8)]

    def rms_to_featmajor(src_tile, src_b, dstT, dst_b, col0, st_tiles, it, part=None):
        junk, junkb, stat, statb, xn, xnb, bk = st_tiles
        c3 = 3 * it
        if part != "b":
            rms_part_a(src_tile, src_b, junk, junkb, stat, statb, xn, xnb, c3)
        if part != "a":
            rms_part_b(dstT, dst_b, col0, xn, xnb, bk, it)

    def rms_part_a(src_tile, src_b, junk, junkb, stat, statb, xn, xnb, c3):
        kb.op("act", lambda e: e.activation(junk[:], src_tile, AF.Square, scale=1.0 / 32.0,
                                            accum_out=stat[:, c3:c3 + 1]),
              reads=[src_b], writes=[junkb, statb])
        kb.op("act", lambda e: e.activation(stat[:, c3 + 1:c3 + 2], stat[:, c3:c3 + 1], AF.Ln,
                                            bias=epsc[:, 0:1]), reads=[epsb], writes=[statb])
        kb.op("act", lambda e: e.activation(stat[:, c3 + 2:c3 + 3], stat[:, c3 + 1:c3 + 2], AF.Exp,
                                            scale=-0.5), writes=[statb])
        kb.op("dve", lambda e: e.tensor_scalar(xn[:], src_tile, stat[:, c3 + 2:c3 + 3], None, ALU.mult),
              reads=[src_b, statb], writes=[xnb])

    def rms_part_b(dstT, dst_b, col0, xn, xnb, bk, it):
        pv16 = bank16(bk, 0, 512)
        kb.group("pe", [(lambda e, c=c: e.transpose(pv16[:, c * 128:(c + 1) * 128], xn[:, c * 128:(c + 1) * 128], identb))
                        for c in range(8)], reads=[xnb, cbb], writes=[bb[bk]])
        if it % 2 == 0:
            kb.op("act", lambda e: e.copy(dstT[:, :, col0:col0 + 128], pv16.rearrange("p (c t) -> p c t", c=8)),
                  reads=[], writes=[bb[bk], dst_b])
        else:
            kb.op("pool", lambda e: e.tensor_copy(xn[:], xn[:]), reads=[], writes=[]) if False else None
            kb.op("dve", lambda e: e.tensor_copy(dstT[:, :, col0:col0 + 128], pv16.rearrange("p (c t) -> p c t", c=8)),
                  reads=[], writes=[bb[bk], dst_b])

    if 0 in phases:
        with contextlib.ExitStack() as st0:
            xst = [kb.sb("xst%d" % i, [128, D], F32, st0) for i in range(2)]
            xstb = [Buf("xst%d" % i) for i in range(2)]
            xn = [kb.sb("xn%d" % i, [128, D], BF16, st0) for i in range(2)]
            xnb = [Buf("xn%d" % i) for i in range(2)]
            junk = [kb.sb("junk0_%d" % i, [128, D], BF16, st0) for i in range(2)]
            junkb = [Buf("junk0_%d" % i) for i in range(2)]
            stat = [kb.sb("stat0_%d" % i, [128, 3 * NCH], F32, st0) for i in range(2)]
            statb = [Buf("stat0_%d" % i) for i in range(2)]
            for i in range(2):
                kb.op("pool", lambda e: e.memset(stat[i][:], 0.0), writes=[statb[i]])
            def p0(tt, part):
                s = tt % 2
                if part == "a":
                    kb.dma("sp", xst[s][:], x_d[tt * 128:(tt + 1) * 128, :], writes=[xstb[s]])
                rms_to_featmajor(xst[s][:], xstb[s], hT, hTb[tt // 4], tt * 128,
                                 (junk[s], junkb[s], stat[s], statb[s], xn[s], xnb[s], tt % 2), tt, part=part)
            p0(0, "a")
            for tt in range(NCH):
                if tt + 1 < NCH:
                    p0(tt + 1, "a")
                p0(tt, "b")
            kb.barrier()
        if dbg:
            dump("hT", hT[:, :, 0:512], [128, 8, 512], BF16, [hTb[0]])

    if 1 in phases:
        with contextlib.ExitStack() as stF:
            rv = kb.sb("rv1", [128, 160], F32, stF)
            rvb = Buf("rv1")
            kb.dma("sp", rv[:], rvec_d[:, 0:160].partition_broadcast(128), writes=[rvb])
            wab = kb.sb("wab", [128, 8, 32], BF16, stF)
            wabb = Buf("wab")
            load_cast(wab, wabb, wab_d, 8, 32, scale_col=P_GMIX)
            gallb, ballb = Buf("gall"), Buf("ball")
            nega = kb.sb("nega", [128, 16], F32, stF)
            negab = Buf("nega")
            kb.op("act", lambda e: e.activation(nega[:], rv[:, R_ALOG:R_ALOG + 16], AF.Exp), reads=[rvb], writes=[negab])
            kb.op("dve", lambda e: e.tensor_scalar(nega[:], nega[:], -1.0, None, ALU.mult), writes=[negab])
            for half in range(2):
                bk = half
                fns = []
                for t16 in range(16):
                    tt = half * 16 + t16
                    for dc in range(8):
                        fns.append(mm(banks[bk][:, t16 * 32:(t16 + 1) * 32], hT[:, dc, tt * 128:(tt + 1) * 128],
                                      wab[:, dc, :], start=(dc == 0), stop=(dc == 7)))
                kb.group("pe", fns, reads=[wabb] + hTb, writes=[bb[bk]])
                ab3 = banks[bk][:, :].rearrange("p (t c) -> p t c", c=32)
                gsl = gall[:, half * 16:(half + 1) * 16, :]
                bsl = ball[:, half * 16:(half + 1) * 16, :]
                dtb_bc = rv[:, R_DTB:R_DTB + 16].unsqueeze(1).to_broadcast([128, 16, 16])
                nega_bc = nega[:, :].unsqueeze(1).to_broadcast([128, 16, 16])
                kb.op("dve", lambda e: e.tensor_tensor(gsl, ab3[:, :, 0:16], dtb_bc, ALU.add),
                      reads=[rvb], writes=[bb[bk], gallb])
                kb.op("act", lambda e: e.activation(bsl, ab3[:, :, 16:32], AF.Tanh, scale=0.5),
                      writes=[bb[bk], ballb])
                kb.op("act", lambda e: e.activation(gsl, gsl, AF.Exp), writes=[gallb])
                kb.op("act", lambda e: e.activation(gsl, gsl, AF.Ln, bias=1.0), writes=[gallb])
                kb.op("dve", lambda e: e.tensor_tensor(gsl, gsl, nega_bc, ALU.mult), reads=[negab], writes=[gallb])
                kb.op("dve", lambda e: e.tensor_scalar(bsl, bsl, 0.5, 0.5, ALU.mult, ALU.add), writes=[ballb])
            if dbg:
                dump("gall", gall[:], [128, NCH, 16], F32, [gallb])
                dump("ball", ball[:], [128, NCH, 16], F32, [ballb])


            wh2 = [kb.sb("wh%d" % i, [128, 8, 512], BF16, stF) for i in range(2)]
            wh2b = [Buf("wh%d" % i) for i in range(2)]
            pre = kb.sb("pre", [128, T + 4], BF16, stF)
            preb = Buf("pre")
            kb.op("pool", lambda e: e.memset(pre[:, 0:2], 0.0), writes=[preb])
            kb.op("pool", lambda e: e.memset(pre[:, T + 2:T + 4], 0.0), writes=[preb])
            dg = kb.sb("dg", [128, 5, 128], BF16, stF)
            dgb = Buf("dg")
            rawv = [kb.sb("rawv%d" % i, [128, 512], BF16, stF) for i in range(2)]
            rawvb = [Buf("rawv%d" % i) for i in range(2)]
            sq_ = [kb.sb("sq%d" % i, [128, 512], BF16, stF) for i in range(2)]
            sqb_ = [Buf("sq%d" % i) for i in range(2)]
            rs_ = [kb.sb("rs%d" % i, [128, 512], F32, stF) for i in range(2)]
            rsb_ = [Buf("rs%d" % i) for i in range(2)]
            hb5 = [[kb.sb("hb%d_%d" % (k, i), [128, T], BF16, stF) for k in range(5)] for i in range(2)]
            hb5b = [[Buf("hb%d_%d" % (k, i)) for k in range(5)] for i in range(2)]

            for h in range(nheads):
                hp = h % 2
                QT, KT = hb5[hp][0], hb5[hp][1]
                QTb, KTb = hb5b[hp][0], hb5b[hp][1]
                Ktok = hb5[hp][2][:, :].rearrange("p (c k) -> p c k", c=NCH)
                Vtok = hb5[hp][3][:, :].rearrange("p (c k) -> p c k", c=NCH)
                Zs = hb5[hp][4][:, :].rearrange("p (c k) -> p c k", c=NCH)
                Ktokb, Vtokb, Zsb = hb5b[hp][2], hb5b[hp][3], hb5b[hp][4]
                wh, whb = wh2[hp], wh2b[hp]
                load_cast(wh, whb, wh_d[h], 8, 512, scale_col=P_GMIX, eng="dve")
                ZsT = hb5[hp][4]
                for t8 in range(8):
                    bk = 6 + t8 % 2
                    kb.group("pe", [mm(banks[bk][:, :], wh[:, dc, 384:512], hT[:, dc, t8 * 512:(t8 + 1) * 512],
                                       start=(dc == 0), stop=(dc == 7)) for dc in range(8)], reads=[whb, hTb[t8]], writes=[bb[bk]])
                    kb.op("act", lambda e: e.activation(ZsT[:, t8 * 512:(t8 + 1) * 512], banks[bk][:, :], AF.Silu),
                          writes=[bb[bk], Zsb])

                def proj_tile(j, t8):
                    bk = 6 + t8 % 2
                    kb.group("pe", [mm(banks[bk][:, :], wh[:, dc, j * 128:(j + 1) * 128],
                                       hT[:, dc, t8 * 512:(t8 + 1) * 512], start=(dc == 0), stop=(dc == 7))
                                    for dc in range(8)], reads=[whb, hTb[t8]], writes=[bb[bk]])
                    kb.op("dve", lambda e: e.tensor_copy(pre[:, 2 + t8 * 512:2 + (t8 + 1) * 512], banks[bk][:, :]),
                          writes=[bb[bk], preb])

                def norm_a(j, t8):
                    dstT, dstb = ((QT, QTb), (KT, KTb))[j]
                    sq, sqb = sq_[t8 % 2], sqb_[t8 % 2]
                    rawt = dstT[:, t8 * 512:(t8 + 1) * 512]
                    kb.op("pool", lambda e: e.tensor_tensor(sq[:], rawt, rawt, ALU.mult), reads=[dstb], writes=[sqb])

                def norm_b(j, t8):
                    dstT, dstb = ((QT, QTb), (KT, KTb))[j]
                    sq, sqb, rs, rsb = sq_[t8 % 2], sqb_[t8 % 2], rs_[t8 % 2], rsb_[t8 % 2]
                    rawt = dstT[:, t8 * 512:(t8 + 1) * 512]
                    bk2 = 4 + (t8 % 2)
                    kb.op("pe", mm(banks[bk2][:, :], onesb, sq[:]), reads=[cbb, sqb], writes=[bb[bk2]])
                    if j == 0:
                        kb.op("act", lambda e: e.activation(rs[:], banks[bk2][:, :], AF.Ln, bias=epsc[:, 1:2], scale=128.0),
                              reads=[epsb], writes=[bb[bk2], rsb])
                    else:
                        kb.op("act", lambda e: e.activation(rs[:], banks[bk2][:, :], AF.Ln, bias=epsc[:, 0:1]),
                              reads=[epsb], writes=[bb[bk2], rsb])
                    kb.op("act", lambda e: e.activation(rs[:], rs[:], AF.Exp, scale=-0.5), writes=[rsb])
                    kb.op("dve", lambda e: e.tensor_tensor(rawt, rawt, rs[:], ALU.mult), reads=[rsb], writes=[dstb])

                for j in range(3):
                    blk = j * 8 + h
                    if j >= 1:
                        norm_a(j - 1, 0)
                    for t8 in range(8):
                        proj_tile(j, t8)
                        if j >= 1:
                            if t8 + 1 < 8:
                                norm_a(j - 1, t8 + 1)
                            norm_b(j - 1, t8)
                    kb.op("pool", lambda e: e.tensor_tensor(dg[:], identb.unsqueeze(1).to_broadcast([128, 5, 128]),
                                                            pv[:, P_QKVW + blk * 5:P_QKVW + blk * 5 + 5].unsqueeze(2).to_broadcast([128, 5, 128]),
                                                            ALU.mult), reads=[cbb, pvb], writes=[dgb])
                    for t8 in range(8):
                        bk = 6 + t8 % 2
                        kb.group("pe", [mm(banks[bk][:, :], dg[:, jj, :], pre[:, t8 * 512 + jj:t8 * 512 + jj + 512],
                                           start=(jj == 0), stop=(jj == 4)) for jj in range(5)],
                                 reads=[dgb, preb], writes=[bb[bk]])
                        if j < 2:
                            dstT, dstb = ((QT, QTb), (KT, KTb))[j]
                            kb.op("act", lambda e: e.activation(dstT[:, t8 * 512:(t8 + 1) * 512], banks[bk][:, :], AF.Silu),
                                  writes=[bb[bk], dstb])
                        else:
                            s_ = t8 % 2
                            kb.op("act", lambda e: e.activation(rawv[s_][:], banks[bk][:, :], AF.Silu), writes=[bb[bk], rawvb[s_]])

                            def vtrans(tv):
                                sv = tv % 2
                                bk2 = 2 + tv % 2
                                v16 = bank16(bk2, 0, 256)
                                kb.group("pe", [(lambda e, jx=jx: e.transpose(v16[:, jx * 128:(jx + 1) * 128], rawv[sv][:, jx * 128:(jx + 1) * 128], identb))
                                                for jx in range(4)], reads=[rawvb[sv], cbb], writes=[bb[bk2]])
                                kb.op("dve", lambda e: e.tensor_copy(Vtok[:, tv * 4:(tv + 1) * 4, :], v16.rearrange("p (j c) -> p j c", j=4)),
                                      writes=[bb[bk2], Vtokb])
                            if t8 >= 1:
                                vtrans(t8 - 1)
                            if t8 == 7:
                                vtrans(7)

                for c8 in range(4):
                    bk = 6 + c8 % 2
                    v16 = bank16(bk, 0, 512)
                    kb.group("pe", [(lambda e, jx=jx: e.transpose(v16[:, jx * 128:(jx + 1) * 128],
                                                                  KT[:, (c8 * 8 + jx) * 128:(c8 * 8 + jx + 1) * 128], identb))
                                    for jx in range(8)], reads=[KTb, cbb], writes=[bb[bk]])
                    kb.op("act", lambda e: e.copy(Ktok[:, c8 * 8:(c8 + 1) * 8, :], v16.rearrange("p (j c) -> p j c", j=8)),
                          writes=[bb[bk], Ktokb])
                if dbg and h == 0:
                    dump("QT", QT[:], [128, T], BF16, [QTb])
                    dump("KT", KT[:], [128, T], BF16, [KTb])
                    dump("Vtok", Vtok[:], [128, NCH, 128], BF16, [Vtokb])
                    dump("Ktok", Ktok[:], [128, NCH, 128], BF16, [Ktokb])

                for k in range(5):
                    kb.dma("pool", hs_d[h, k], hb5[hp][k][:], reads=[hb5b[hp][k]], writes=[hsdb[h]])
            kb.barrier()
    if 2 in phases:
        with contextlib.ExitStack() as st2:
            extra_stg(st2, 6, "p2")
            wglu = kb.sb("wglu", [128, 8, 2048], BF16, st2)
            wglub = [Buf("wglu%d" % i) for i in range(4)]
            load_cast(wglu, wglub, wglu_d, 8, 2048, scale_col=P_GMIX, order=[0, 2, 1, 3])
            crow = [kb.sb("crow%d" % i, [128, T + 30], BF16, st2) for i in range(2)]
            crowb = [Buf("crow%d" % i) for i in range(2)]
            tb_ = [kb.sb("tbg%d" % i, [128, 512], F32, st2) for i in range(2)]
            tbb = [Buf("tbg%d" % i) for i in range(2)]
            for i in range(2):
                kb.op("pool", lambda e: e.memset(crow[i][:, 0:15], 0.0), writes=[crowb[i]])
                kb.op("pool", lambda e: e.memset(crow[i][:, T + 15:T + 30], 0.0), writes=[crowb[i]])
            n = 0
            for cbk in range(8):
                r = cbk % 2
                for t8 in range(8):
                    s = n % 2
                    ba, bg = 2 * s, 2 * s + 1
                    n += 1
                    kb.group("pe", [mm(banks[ba][:, :], wglu[:, dc, cbk * 128:(cbk + 1) * 128],
                                       hT[:, dc, t8 * 512:(t8 + 1) * 512], start=(dc == 0), stop=(dc == 7)) for dc in range(8)],
                             reads=[wglub[cbk // 4], hTb[t8]], writes=[bb[ba]])
                    kb.group("pe", [mm(banks[bg][:, :], wglu[:, dc, 1024 + cbk * 128:1024 + (cbk + 1) * 128],
                                       hT[:, dc, t8 * 512:(t8 + 1) * 512], start=(dc == 0), stop=(dc == 7)) for dc in range(8)],
                             reads=[wglub[2 + cbk // 4], hTb[t8]], writes=[bb[bg]])
                    kb.op("act", lambda e: e.activation(tb_[s][:], banks[bg][:, :], AF.Sigmoid),
                          writes=[bb[bg], tbb[s]])
                    kb.op("dve", lambda e: e.tensor_tensor(crow[r][:, 15 + t8 * 512:15 + (t8 + 1) * 512], banks[ba][:, :],
                                                           tb_[s][:], ALU.mult),
                          reads=[tbb[s]], writes=[bb[ba], crowb[r]])
                kb.dma("pool", cT_d[cbk], crow[r][:], reads=[crowb[r]], writes=[cTb])
            wgate = kb.sb("wgate", [128, 8, 2048], BF16, st2)
            wgateb = [Buf("wgate%d" % i) for i in range(4)]
            load_cast(wgate, wgateb, wgate_d, 8, 2048, scale_col=P_GMIX)
            grow = [kb.sb("grow%d" % i, [128, T], BF16, st2) for i in range(2)]
            growb = [Buf("grow%d" % i) for i in range(2)]
            for gblk in range(16):
                r = gblk % 2
                for t8 in range(8):
                    bk = 4 + (t8 % 4)
                    kb.group("pe", [mm(banks[bk][:, :], wgate[:, dc, gblk * 128:(gblk + 1) * 128],
                                       hT[:, dc, t8 * 512:(t8 + 1) * 512], start=(dc == 0), stop=(dc == 7)) for dc in range(8)],
                             reads=[wgateb[gblk // 4], hTb[t8]], writes=[bb[bk]])
                    kb.op("act", lambda e: e.activation(grow[r][:, t8 * 512:(t8 + 1) * 512], banks[bk][:, :], AF.Sigmoid),
                          writes=[bb[bk], growb[r]])
                kb.dma("pool", gT_d[gblk], grow[r][:], reads=[growb[r]], writes=[gTb])
            kb.barrier()
    stA.close()
    if 1 in phases:
        drop_extra_stg()
        with contextlib.ExitStack() as st1:
            pre = kb.sb("preL", [128, T + 4], BF16, st1)
            preb = Buf("pre")
            kb.op("pool", lambda e: e.memset(pre[:, 0:2], 0.0), writes=[preb])
            kb.op("pool", lambda e: e.memset(pre[:, T + 2:T + 4], 0.0), writes=[preb])
            oacc2 = [kb.sb("oacc_%d" % j, [128, NCH, 128], F32, st1) for j in range(2)]
            oaccb2 = [[Buf("oacc%d_%d" % (j, i)) for i in range(NCH)] for j in range(2)]
            tabs2 = [{}, {}]
            for j in range(2):
                for nm in ("gsel", "bsel", "gc", "ngc", "egc", "cw", "ktl", "egl"):
                    tabs2[j][nm] = kb.sb("t_%s%d" % (nm, j), [128, NCH, 2], F32, st1)
            hsb2 = [Buf("headscal%d" % j) for j in range(2)]
            ssq = kb.sb("ssq", [128, 2 * NCH], F32, st1)
            ssqb = Buf("ssq")
            NSLOT = 6
            def mk(nm, shape, dt):
                return [kb.sb("%s_%d" % (nm, q), shape, dt, st1) for q in range(NSLOT)]
            def mkb(nm):
                return [Buf("%s_%d" % (nm, q)) for q in range(NSLOT)]
            G2_ = [kb.sb("G2_%d" % i, [128, 2, 128], F32, st1) for i in range(2)] * 3
            G2_b = [Buf("G2_%d" % i) for i in range(2)] * 3
            Gt2_ = [kb.sb("Gt2_%d" % i, [128, 2, 128], F32, st1) for i in range(2)] * 3
            Gt2_b = [Buf("Gt2_%d" % i) for i in range(2)] * 3
            nE2_ = [kb.sb("nE2_%d" % i, [128, 256], F32, st1) for i in range(2)] * 3
            nE2_b = [Buf("nE2_%d" % i) for i in range(2)] * 3
            F2_, F2_b = mk("F2", [128, 256], BF16), mkb("F2")
            Fb2_, Fb2_b = mk("Fb2", [128, 256], BF16), mkb("Fb2")
            Fm2_, Fm2_b = mk("Fm2", [128, 256], BF16), mkb("Fm2")
            AA4_, AA4_b = mk("AA4", [128, 512], BF16), mkb("AA4")
            qk2_, qk2_b = mk("qk2", [128, 256], BF16), mkb("qk2")
            rv2_, rv2_b = mk("rv2", [128, 256], BF16), mkb("rv2")
            rw2_, rw2_b = mk("rw2", [128, 256], BF16), mkb("rw2")
            kt2_, kt2_b = mk("kt2", [128, 256], BF16), mkb("kt2")
            nZ4_, nZ4_b = mk("nZ4", [128, 512], BF16), mkb("nZ4")
            TR4_, TR4_b = mk("TR4", [128, 512], BF16), mkb("TR4")
            Lm4_, Lm4_b = mk("Lm4", [128, 512], BF16), mkb("Lm4")
            nwT = [kb.sb("nwT%d" % d, [128, 128], BF16, st1) for d in range(2)]
            qs_ = [kb.sb("qs%d" % d, [128, 128], BF16, st1) for d in range(2)]
            vn = [kb.sb("vn%d" % d, [128, 128], BF16, st1) for d in range(2)]
            nwTb = [Buf("nwT%d" % d) for d in range(2)]
            qsb = [Buf("qs%d" % d) for d in range(2)]
            vnb = [Buf("vn%d" % d) for d in range(2)]
            S32 = [kb.sb("S32_%d" % d, [128, 128], F32, st1) for d in range(2)]
            S16 = [kb.sb("S16_%d" % d, [128, 128], BF16, st1) for d in range(2)]
            S32b = [Buf("S32_%d" % d) for d in range(2)]
            S16b = [Buf("S16_%d" % d) for d in range(2)]
            tri2 = cf[:, C_TL:C_TL + 256].rearrange("p (a c) -> p a c", a=2)
            ones2 = cf[:, C_ONE:C_ONE + 128].unsqueeze(1).to_broadcast([128, 2, 128])
            sg2 = cf[:, C_SGF:C_SGF + 256]
            negone = cf[:, C_NEG:C_NEG + 128]
            mask2 = cb[:, C_MUP:C_MUP + 256]


            hb5 = [[kb.sb("lb%d_%d" % (k, i), [128, T], BF16, st1) for k in range(5)] for i in range(2)]
            hb5b = [[Buf("lb%d_%d" % (k, i)) for k in range(5)] for i in range(2)]

            def load_head(hh):
                for k in range(5):
                    kb.dma("sp", hb5[hh % 2][k][:], hs_d[hh, k], reads=[hsdb[hh]], writes=[hb5b[hh % 2][k]])

            def make_head(h):
                hp = h % 2
                tabs, hsb = tabs2[hp], hsb2[hp]
                oacc, oaccb = oacc2[hp], oaccb2[hp]
                QT, KT = hb5[hp][0], hb5[hp][1]
                QTb, KTb = hb5b[hp][0], hb5b[hp][1]
                Ktok = hb5[hp][2][:, :].rearrange("p (c k) -> p c k", c=NCH)
                Vtok = hb5[hp][3][:, :].rearrange("p (c k) -> p c k", c=NCH)
                Zs = hb5[hp][4][:, :].rearrange("p (c k) -> p c k", c=NCH)
                Ktokb, Vtokb, Zsb = hb5b[hp][2], hb5b[hp][3], hb5b[hp][4]
                og, ogb = Ktok, Ktokb

                def prologue(bk):
                    for d in range(2):
                        col = d * 8 + h
                        gsrc = gall[:, :, col] if d == 0 else gall[:, ::-1, col]
                        bsrc = ball[:, :, col] if d == 0 else ball[:, ::-1, col]
                        kb.op("pool", lambda e: e.tensor_copy(tabs["gsel"][:, :, d], gsrc), reads=[gallb], writes=[hsb])
                        kb.op("pool", lambda e: e.tensor_copy(tabs["bsel"][:, :, d], bsrc), reads=[ballb], writes=[hsb])
                    kb.group("pe", [
                        mm(banks[bk][:, 0:32], cf[:, C_TL:C_TL + 128], tabs["gsel"][:, :, 0]),
                        mm(banks[bk][:, 32:64], cf[:, C_TG:C_TG + 128], tabs["gsel"][:, :, 1]),
                        mm(banks[bk][:, 64:96], cf[:, C_ONE:C_ONE + 128], tabs["gsel"][:, :, 0]),
                        mm(banks[bk][:, 96:128], cf[:, C_ONE:C_ONE + 128], tabs["gsel"][:, :, 1]),
                    ], reads=[hsb, cfb], writes=[bb[bk]])
                    for d in range(2):
                        gcp = banks[bk][:, d * 32:(d + 1) * 32]
                        glp = banks[bk][:, 64 + d * 32:64 + (d + 1) * 32]
                        kb.op("dve", lambda e: e.tensor_copy(tabs["gc"][:, :, d], gcp), writes=[bb[bk], hsb])
                        kb.op("act", lambda e: e.activation(tabs["egc"][:, :, d], gcp, AF.Exp), writes=[bb[bk], hsb])
                        kb.op("dve", lambda e: e.tensor_tensor(tabs["ktl"][:, :, d], glp, tabs["gc"][:, :, d], ALU.subtract),
                              writes=[bb[bk], hsb])
                        kb.op("act", lambda e: e.activation(tabs["egl"][:, :, d], glp, AF.Exp), writes=[bb[bk], hsb])
                    kb.op("dve", lambda e: e.tensor_tensor(tabs["cw"][:], tabs["egc"][:], tabs["bsel"][:], ALU.mult), writes=[hsb])
                    kb.op("dve", lambda e: e.tensor_scalar(tabs["ngc"][:], tabs["gc"][:], -1.0, None, ALU.mult), writes=[hsb])
                    kb.op("act", lambda e: e.activation(tabs["ktl"][:], tabs["ktl"][:], AF.Exp), writes=[hsb])


                def state_init():
                    for d in range(2):
                        kb.op("pool", lambda e: e.memset(S32[d][:], 0.0), writes=[S32b[d]])
                        kb.op("pool", lambda e: e.memset(S16[d][:], 0.0), writes=[S16b[d]])


                def pair_stages(it, q):
                    pbk = q
                    cc = (it, NCH - 1 - it)
                    G2, G2b = G2_[q], G2_b[q]
                    Gt2, Gt2b = Gt2_[q], Gt2_b[q]
                    nE2, nE2b = nE2_[q], nE2_b[q]
                    F2, F2b = F2_[q], F2_b[q]
                    Fb2, Fb2b = Fb2_[q], Fb2_b[q]
                    Fm2, Fm2b = Fm2_[q], Fm2_b[q]
                    AA4, AA4b = AA4_[q], AA4_b[q]
                    qk2, qk2b = qk2_[q], qk2_b[q]
                    rv2, rv2b = rv2_[q], rv2_b[q]
                    rw2, rw2b = rw2_[q], rw2_b[q]
                    kt2, kt2b = kt2_[q], kt2_b[q]
                    nZ4, nZ4b = nZ4_[q], nZ4_b[q]
                    TR4, TR4b = TR4_[q], TR4_b[q]
                    Lm4, Lm4b = Lm4_[q], Lm4_b[q]
                    pbank = banks[pbk]
                    bsc = lambda nm: tabs[nm][:, it, :].unsqueeze(2).to_broadcast([128, 2, 128])
                    L_op = (AA4[:, 0:128], AA4[:, 384:512])
                    LT_op = (AA4[:, 256:384], AA4[:, 128:256])
                    st = []

                    def p0():
                        kb.op("pool", lambda e: e.tensor_tensor(G2[:], cf[:, C_ID:C_ID + 128].unsqueeze(1).to_broadcast([128, 2, 128]),
                                                                bsc("gc"), ALU.mult), reads=[cfb, hsb], writes=[G2b])
                        fns = [mm(pbank[:, 0:256], cf[:, C_ONE:C_ONE + 128], G2[:, :, :].rearrange("p a c -> p (a c)"))]
                        for x in range(2):
                            c0 = cc[x] * 128
                            fns.append(mm(pbank[:, 256 + x * 128:256 + (x + 1) * 128], KT[:, c0:c0 + 128], KT[:, c0:c0 + 128]))
                        kb.group("pe", fns, reads=[G2b, cfb, KTb], writes=[bb[pbk]])
                    st.append(p0)

                    def p1():
                        for x in range(2):
                            kb.op("act", lambda e: e.activation(nE2[:, x * 128:(x + 1) * 128], pbank[:, x * 128:(x + 1) * 128], AF.Abs,
                                                                bias=tabs["ngc"][:, it, x:x + 1]),
                                  reads=[hsb], writes=[bb[pbk], nE2b])
                    st.append(p1)

                    def p2():
                        kb.op("act", lambda e: e.activation(F2[:], nE2[:], AF.Exp, scale=-1.0), reads=[nE2b], writes=[F2b])
                    st.append(p2)

                    def p3():
                        f3 = F2[:, :].rearrange("p (a c) -> p a c", a=2)
                        kb.op("pool", lambda e: e.tensor_tensor(Fb2[:, :].rearrange("p (a c) -> p a c", a=2), f3, bsc("bsel"), ALU.mult),
                              reads=[F2b, hsb], writes=[Fb2b])
                        kb.op("pool", lambda e: e.tensor_tensor(Fm2[:], F2[:], mask2, ALU.mult), reads=[F2b, cbb], writes=[Fm2b])
                        for x in range(2):
                            c = cc[x]
                            xs = slice(x * 128, (x + 1) * 128)
                            bc1 = lambda nm: tabs[nm][:, it, x:x + 1].to_broadcast([128, 128])
                            kb.op("pool", lambda e: e.tensor_tensor(rv2[:, xs], Vtok[:, c, :], bc1("bsel"), ALU.mult),
                                  reads=[Vtokb, hsb], writes=[rv2b])
                            kb.op("pool", lambda e: e.tensor_tensor(rw2[:, xs], Ktok[:, c, :], bc1("cw"), ALU.mult),
                                  reads=[Ktokb, hsb], writes=[rw2b])
                            kb.op("pool", lambda e: e.tensor_tensor(kt2[:, xs], Ktok[:, c, :], bc1("ktl"), ALU.mult),
                                  reads=[Ktokb, hsb], writes=[kt2b])
                    st.append(p3)

                    def p4():
                        kb.op("dve", lambda e: e.tensor_tensor(AA4[:, 0:256], pbank[:, 256:512], Fb2[:], ALU.mult),
                              reads=[Fb2b], writes=[bb[pbk], AA4b])
                    st.append(p4)

                    def p5():
                        at16 = bank16(pbk, 256, 384)
                        fns = []
                        for x in range(2):
                            c0 = cc[x] * 128
                            fns.append(mm(pbank[:, x * 128:(x + 1) * 128], KT[:, c0:c0 + 128], QT[:, c0:c0 + 128]))
                            fns.append(lambda e, x=x: e.transpose(at16[:, x * 128:(x + 1) * 128], AA4[:, x * 128:(x + 1) * 128], identb))
                        kb.group("pe", fns, reads=[KTb, QTb, AA4b, cbb], writes=[bb[pbk]])
                    st.append(p5)

                    def p6():
                        kb.op("dve", lambda e: e.tensor_tensor(qk2[:], pbank[:, 0:256], Fm2[:], ALU.mult), reads=[Fm2b], writes=[bb[pbk], qk2b])
                        kb.op("act", lambda e: e.copy(AA4[:, 256:512], bank16(pbk, 256, 384)), writes=[bb[pbk], AA4b])
                    st.append(p6)

                    ACT_LEVELS = (2, 4)
                    for l in range(7):
                        lv = cb[:, C_LV + l * 256:C_LV + (l + 1) * 256]
                        Tc = (identb, identb) if l == 0 else (TR4[:, 0:128], TR4[:, 256:384])
                        Rc = (identb, identb) if l == 0 else (TR4[:, 128:256], TR4[:, 384:512])
                        rd = [AA4b, cbb] + ([] if l == 0 else [TR4b])
                        if l < 6:
                            if l == 0:
                                def za():
                                    pass
                                def zb(lv=lv):
                                    aa = AA4[:, :].rearrange("p (b c) -> p b c", b=4)
                                    nz = nZ4[:, :].rearrange("p (b c) -> p b c", b=4)
                                    lvm = lv.rearrange("p (b c) -> p b c", b=2)
                                    kb.op("pool", lambda e: e.tensor_tensor(nz[:, 0:2, :], aa[:, 0:4:2, :], lvm, ALU.mult),
                                          reads=[AA4b, cbb], writes=[nZ4b])
                                    kb.op("pool", lambda e: e.tensor_tensor(nz[:, 2:4, :], aa[:, 3:0:-2, :], lvm, ALU.mult),
                                          reads=[AA4b, cbb], writes=[nZ4b])
                            elif l in POOL_LV:
                                def za(Tc=Tc, Rc=Rc, rd=rd):
                                    LmT = (Lm4[:, 256:384], Lm4[:, 128:256])
                                    Lm = (Lm4[:, 0:128], Lm4[:, 384:512])
                                    fns = []
                                    for x in range(2):
                                        fns.append(mm(pbank[:, x * 256:x * 256 + 128], LmT[x], Tc[x]))
                                        fns.append(mm(pbank[:, x * 256 + 128:x * 256 + 256], Lm[x], Rc[x]))
                                    kb.group("pe", fns, reads=[Lm4b, TR4b], writes=[bb[pbk]])
                                def zb(lv=lv):
                                    kb.op("act", lambda e: e.copy(nZ4[:], pbank[:, :]), writes=[bb[pbk], nZ4b])
                            else:
                                def za(Tc=Tc, Rc=Rc, rd=rd):
                                    fns = []
                                    for x in range(2):
                                        fns.append(mm(pbank[:, x * 256:x * 256 + 128], LT_op[x], Tc[x]))
                                        fns.append(mm(pbank[:, x * 256 + 128:x * 256 + 256], L_op[x], Rc[x]))
                                    kb.group("pe", fns, reads=rd, writes=[bb[pbk]])
                                def zb(lv=lv):
                                    kb.op("dve", lambda e: e.tensor_tensor(nZ4[:, :].rearrange("p (a c) -> p a c", a=2),
                                                                           pbank[:, :].rearrange("p (a c) -> p a c", a=2),
                                                                           lv.unsqueeze(1).to_broadcast([128, 2, 256]), ALU.mult),
                                          reads=[cbb], writes=[bb[pbk], nZ4b])
                            def zc(Tc=Tc, Rc=Rc, rd=rd, l=l):
                                if l == 0:
                                    return
                                fns = []
                                for x in range(2):
                                    o = x * 256
                                    if l in ACT_LEVELS:
                                        fns += [mm(pbank[:, o:o + 128], Rc[x], identb, start=True, stop=False),
                                                mm(pbank[:, o:o + 128], Rc[x], nZ4[:, o:o + 128], start=False, stop=True),
                                                mm(pbank[:, o + 128:o + 256], Tc[x], identb, start=True, stop=False),
                                                mm(pbank[:, o + 128:o + 256], Tc[x], nZ4[:, o + 128:o + 256], start=False, stop=True)]
                                    else:
                                        fns += [mm(pbank[:, o:o + 128], Rc[x], nZ4[:, o:o + 128]),
                                                mm(pbank[:, o + 128:o + 256], Tc[x], nZ4[:, o + 128:o + 256])]
                                kb.group("pe", fns, reads=rd + [nZ4b], writes=[bb[pbk]])
                                if (l + 1) in POOL_LV:
                                    lvq = cb[:, C_LVQ + POOL_LV.index(l + 1) * 512:C_LVQ + (POOL_LV.index(l + 1) + 1) * 512]
                                    kb.op("pool", lambda e: e.tensor_tensor(Lm4[:], AA4[:], lvq, ALU.mult), reads=[AA4b, cbb], writes=[Lm4b])
                            def zd(l=l):
                                if l in ACT_LEVELS:
                                    kb.op("act", lambda e: e.copy(TR4[:], pbank[:, :]), writes=[bb[pbk], TR4b])
                                elif l == 0:
                                    kb.op("dve", lambda e: e.tensor_tensor(TR4[:, :].rearrange("p (b c) -> p b c", b=4),
                                                                           nZ4[:, :].rearrange("p (b c) -> p b c", b=4),
                                                                           identb.unsqueeze(1).to_broadcast([128, 4, 128]), ALU.add),
                                          reads=[cbb, nZ4b], writes=[TR4b])
                                else:
                                    kb.op("dve", lambda e: e.tensor_tensor(TR4[:], pbank[:, :], TR4[:], ALU.add), writes=[bb[pbk], TR4b])
                        else:
                            def za(Tc=Tc, Rc=Rc, rd=rd):
                                kb.group("pe", [mm(pbank[:, 128:256], L_op[0], Rc[0]), mm(pbank[:, 256:384], LT_op[1], Tc[1])],
                                         reads=rd, writes=[bb[pbk]])
                            def zb(lv=lv):
                                kb.op("dve", lambda e: e.tensor_tensor(nZ4[:, 128:384], pbank[:, 128:384], cb[:, C_LV6X:C_LV6X + 256], ALU.mult),
                                      reads=[cbb], writes=[bb[pbk], nZ4b])
                            def zc(Tc=Tc, Rc=Rc, rd=rd):
                                kb.group("pe", [mm(pbank[:, 128:256], Tc[0], nZ4[:, 128:256]),
                                                mm(pbank[:, 256:384], Rc[1], nZ4[:, 256:384])],
                                         reads=rd + [nZ4b], writes=[bb[pbk]])
                            def zd():
                                kb.op("dve", lambda e: e.tensor_tensor(TR4[:, 128:384], pbank[:, 128:384], TR4[:, 128:384], ALU.add),
                                      writes=[bb[pbk], TR4b])
                        st += [za, zb, zc, zd]
                    n_pre = len(st)
                    Rf = (TR4[:, 128:256], TR4[:, 256:384])

                    def s1():
                        for d in range(2):
                            sbank = banks[6 + d]
                            c0 = cc[d] * 128
                            kb.group("pe", [mm(sbank[:, 0:128], rw2[:, d * 128:(d + 1) * 128], Rf[d]),
                                            mm(sbank[:, 128:256], QT[:, c0:c0 + 128], S16[d][:])],
                                     reads=[rw2b, TR4b, QTb, S16b[d]], writes=[bb[6 + d]])
                    def s2():
                        for d in range(2):
                            sbank = banks[6 + d]
                            kb.op("act", lambda e: e.activation(nwT[d][:], sbank[:, 0:128], AF.Copy, scale=-1.0), writes=[bb[6 + d], nwTb[d]])
                            kb.op("act", lambda e: e.activation(qs_[d][:], sbank[:, 128:256], AF.Copy, scale=tabs["egc"][:, it, d:d + 1]),
                                  reads=[hsb], writes=[bb[6 + d], qsb[d]])
                    def s3():
                        for d in range(2):
                            sbank = banks[6 + d]
                            kb.group("pe", [mm(sbank[:, 256:384], Rf[d], rv2[:, d * 128:(d + 1) * 128], start=True, stop=False),
                                            mm(sbank[:, 256:384], nwT[d][:], S16[d][:], start=False, stop=True)],
                                     reads=[TR4b, rv2b, nwTb[d], S16b[d]], writes=[bb[6 + d]])
                    def s4():
                        for d in range(2):
                            sbank = banks[6 + d]
                            kb.op("act", lambda e: e.copy(vn[d][:], sbank[:, 256:384]), writes=[bb[6 + d], vnb[d]])
                    def s5():
                        for d in range(2):
                            sbank = banks[6 + d]
                            kb.group("pe", [mm(sbank[:, 384:512], identb, qs_[d][:], start=True, stop=False),
                                            mm(sbank[:, 384:512], qk2[:, d * 128:(d + 1) * 128], vn[d][:], start=False, stop=True),
                                            mm(sbank[:, 0:128], kt2[:, d * 128:(d + 1) * 128], vn[d][:])],
                                     reads=[cbb, qsb[d], qk2b, vnb[d], kt2b], writes=[bb[6 + d]])
                    def s6():
                        for d in range(2):
                            sbank = banks[6 + d]
                            c = cc[d]
                            eglc = tabs["egl"][:, it, d:d + 1]
                            kb.op("dve", lambda e: e.scalar_tensor_tensor(S32[d][:], S32[d][:], eglc, sbank[:, 0:128], ALU.mult, ALU.add),
                                  reads=[hsb], writes=[bb[6 + d], S32b[d]])
                            if it < NCH // 2:
                                kb.op("act", lambda e: e.copy(oacc[:, c, :], sbank[:, 384:512]), writes=[bb[6 + d], oaccb[c]])
                            else:
                                kb.op("dve", lambda e: e.tensor_tensor(oacc[:, c, :], sbank[:, 384:512], oacc[:, c, :], ALU.add),
                                      writes=[bb[6 + d], oaccb[c]])
                    def s7():
                        for d in range(2):
                            kb.op("act", lambda e: e.copy(S16[d][:], S32[d][:]), reads=[S32b[d]], writes=[S16b[d]])
                    st += [s1, s2, s3, s4, s5, s6, s7]
                    return st, n_pre

                def epilogue():
                    if dbg and h == 0:
                        dump("oacc", oacc[:], [128, NCH, 128], F32, oaccb)
                    prej = pre[:, 2:T + 2].rearrange("p (c k) -> p c k", c=NCH)
                    kb.op("dve", lambda e: e.tensor_tensor(prej, oacc[:], oacc[:], ALU.mult), reads=oaccb, writes=[preb])
                    kb.op("dve", lambda e: e.tensor_reduce(ssq[:, 0:NCH], prej, AX.X, ALU.add), reads=[preb], writes=[ssqb])
                    kb.op("act", lambda e: e.activation(ssq[:, NCH:2 * NCH], ssq[:, 0:NCH], AF.Ln, bias=epsc[:, 0:1], scale=1.0 / 128.0),
                          reads=[epsb], writes=[ssqb])
                    kb.op("act", lambda e: e.activation(ssq[:, NCH:2 * NCH], ssq[:, NCH:2 * NCH], AF.Exp, scale=-0.5), writes=[ssqb])
                    rs_bc = ssq[:, NCH:2 * NCH].unsqueeze(2).to_broadcast([128, NCH, 128])
                    kb.op("dve", lambda e: e.tensor_tensor(prej, oacc[:], rs_bc, ALU.mult), reads=oaccb + [ssqb], writes=[preb])
                    ZsT = hb5[hp][4]
                    oTs = hb5[hp][2]
                    for c8 in range(4):
                        bk = 6 + c8 % 2
                        v16 = bank16(bk, 0, 512)
                        kb.group("pe", [(lambda e, j=j: e.transpose(v16[:, j * 128:(j + 1) * 128], prej[:, c8 * 8 + j, :], identb))
                                        for j in range(8)], reads=[preb, cbb], writes=[bb[bk]])
                        kb.op("dve", lambda e: e.tensor_tensor(oTs[:, c8 * 1024:(c8 + 1) * 1024], v16, ZsT[:, c8 * 1024:(c8 + 1) * 1024], ALU.mult),
                              reads=[Zsb], writes=[bb[bk], Ktokb])
                    if dbg and h == 0:
                        dump("ogT", oTs[:, :], [128, T], BF16, [Ktokb])
                    kb.dma("pool", oT_d[h], oTs[:, :], reads=[Ktokb], writes=[oTb[h]])

                return prologue, state_init, pair_stages, epilogue

            heads = [make_head(h) for h in range(nheads)]
            stream = [(h, it) for h in range(nheads) for it in range(NCH)]
            load_head(0)
            if nheads > 1:
                load_head(1)
            active = []
            nxt = 0
            scan_done = -1
            tick = 0
            STAG = 7
            while nxt < len(stream) or active:
                if nxt < len(stream) and len(active) < NSLOT and tick % STAG == 0:
                    h, it = stream[nxt]
                    if it == 0:
                        heads[h][0](nxt % NSLOT)
                    stg_list, n_pre = heads[h][2](it, nxt % NSLOT)
                    active.append([stg_list, 0, nxt, n_pre])
                    nxt += 1
                for a_ in list(active):
                    if a_[1] >= a_[3] and a_[2] != scan_done + 1:
                        continue
                    if a_[1] == a_[3] and stream[a_[2]][1] == 0:
                        heads[stream[a_[2]][0]][1]()
                    a_[0][a_[1]]()
                    a_[1] += 1
                    if a_[1] == len(a_[0]):
                        active.remove(a_)
                        scan_done = a_[2]
                        if stream[a_[2]][1] == NCH - 1:
                            hd = stream[a_[2]][0]
                            heads[hd][3]()
                            if hd + 2 < nheads:
                                load_head(hd + 2)
                tick += 1
            kb.barrier()


    if 3 in phases:
        with contextlib.ExitStack() as st3:
            extra_stg(st3, 4, "p3")
            dgc = kb.sb("dgc", [128, 8 * CK, 128], BF16, st3)
            dgcb = [Buf("dgc%d" % i) for i in range(8)]
            for cbk in range(8):
                kb.op("pool", lambda e: e.tensor_tensor(dgc[:, cbk * CK:(cbk + 1) * CK, :], identb.unsqueeze(1).to_broadcast([128, CK, 128]),
                                                        pv[:, P_CW + cbk * CK:P_CW + (cbk + 1) * CK].unsqueeze(2).to_broadcast([128, CK, 128]),
                                                        ALU.mult), reads=[cbb, pvb], writes=[dgcb[cbk]])
            wco = kb.sb("wco", [128, 8, D], BF16, st3)
            wcob = Buf("wco")
            load_cast(wco, wcob, wco_d, 8, D)
            cwin = [kb.sb("cwin%d" % i, [128, 8, 542], BF16, st3) for i in range(2)]
            cwinb = [Buf("cwin%d" % i) for i in range(2)]
            cv = kb.sb("cv", [128, 8, 512], BF16, st3)
            cvb = Buf("cv")
            sqv = [kb.sb("sqv%d" % i, [128, 512], BF16, st3) for i in range(2)]
            sqvb = [Buf("sqv%d" % i) for i in range(2)]
            mean = kb.sb("mean", [128, 512], F32, st3)
            msq = kb.sb("msq", [128, 512], F32, st3)
            rstd = kb.sb("rstd", [128, 512], F32, st3)
            stb = Buf("lnstat")
            t1 = [kb.sb("t1_%d" % i, [128, 512], F32, st3) for i in range(2)]
            t1b = [Buf("t1_%d" % i) for i in range(2)]
            csl = kb.sb("csl", [128, 8, 512], BF16, st3)
            cslb = Buf("csl")
            ycs = [kb.sb("ycs%d" % i, [128, 8, 512], BF16, st3) for i in range(2)]
            ycsb = [Buf("ycs%d" % i) for i in range(2)]
            cv2 = [cv, kb.sb("cvB", [128, 8, 512], BF16, st3)]
            cv2b = [cvb, Buf("cvB")]

            def conv_blocks(t8, cbks):
                w = t8 % 2
                cvx, cvxb = cv2[w], cv2b[w]
                if cbks[0] == 0:
                    kb.dma("sp", cwin[w][:], cT_d[:, :, t8 * 512:t8 * 512 + 542].rearrange("c p t -> p c t"),
                           reads=[cTb], writes=[cwinb[w]])

                def stats_mm(cb_):
                    kb.op("pe", mm(banks[2][:, :], onesb, cvx[:, cb_, :], start=(cb_ == 0), stop=(cb_ == 7)),
                          reads=[cbb, cvxb], writes=[bb[2]])
                    kb.op("pe", mm(banks[3][:, :], onesb, sqv[cb_ % 2][:], start=(cb_ == 0), stop=(cb_ == 7)),
                          reads=[cbb, sqvb[cb_ % 2]], writes=[bb[3]])
                for cbk in cbks:
                    bk = cbk % 2
                    s = cbk % 2
                    kb.group("pe", [mm(banks[bk][:, :], dgc[:, cbk * CK + jj, :], cwin[w][:, cbk, jj:jj + 512],
                                       start=(jj == 0), stop=(jj == CK - 1)) for jj in range(CK)],
                             reads=[dgcb[cbk], cwinb[w]], writes=[bb[bk]])
                    bcol = pv[:, P_CB + cbk:P_CB + cbk + 1]
                    kb.op("act", lambda e: e.activation(cvx[:, cbk, :], banks[bk][:, :], AF.Identity, bias=bcol),
                          reads=[pvb], writes=[bb[bk], cvxb])
                    kb.op("act", lambda e: e.activation(sqv[s][:], banks[bk][:, :], AF.Square, bias=bcol),
                          reads=[pvb], writes=[bb[bk], sqvb[s]])
                    if cbk >= 1:
                        stats_mm(cbk - 1)
                if cbks[-1] == 7:
                    stats_mm(7)

            def ln_part(t8):
                w = t8 % 2
                cvx, cvxb = cv2[w], cv2b[w]
                kb.op("dve", lambda e: e.tensor_scalar(mean[:], banks[2][:, :], 1.0 / 1024.0, None, ALU.mult),
                      writes=[bb[2], stb])
                kb.op("dve", lambda e: e.tensor_tensor(msq[:], mean[:], mean[:], ALU.mult), writes=[stb])
                kb.op("dve", lambda e: e.scalar_tensor_tensor(msq[:], banks[3][:, :], 1.0 / 1024.0, msq[:], ALU.mult, ALU.subtract),
                      writes=[bb[3], stb])
                kb.op("act", lambda e: e.activation(rstd[:], msq[:], AF.Ln, bias=epsc[:, 2:3]), reads=[epsb], writes=[stb])
                kb.op("act", lambda e: e.activation(rstd[:], rstd[:], AF.Exp, scale=-0.5), writes=[stb])
                for cbk in range(8):
                    s = cbk % 2
                    kb.op("dve", lambda e: e.tensor_tensor(t1[s][:], cvx[:, cbk, :], mean[:], ALU.subtract),
                          reads=[cvxb, stb], writes=[t1b[s]])
                    kb.op("pool", lambda e: e.tensor_tensor(t1[s][:], t1[s][:], rstd[:], ALU.mult),
                          reads=[stb], writes=[t1b[s]])
                    kb.op("act", lambda e: e.activation(csl[:, cbk, :], t1[s][:], AF.Silu,
                                                        bias=pv[:, P_LNB + cbk:P_LNB + cbk + 1],
                                                        scale=pv[:, P_LNG + cbk:P_LNG + cbk + 1]),
                          reads=[t1b[s], pvb], writes=[cslb])

            def yconv(t8):
                w = t8 % 2
                for nb in range(8):
                    bk = 4 + (nb % 2)
                    kb.group("pe", [mm(banks[bk][:, :], wco[:, cbk, nb * 128:(nb + 1) * 128], csl[:, cbk, :],
                                       start=(cbk == 0), stop=(cbk == 7)) for cbk in range(8)],
                             reads=[wcob, cslb], writes=[bb[bk]])
                    kb.op("dve", lambda e: e.tensor_copy(ycs[w][:, nb, :], banks[bk][:, :]), writes=[bb[bk], ycsb[w]])
                kb.dma("pool", ycT_d[:, :, t8 * 512:(t8 + 1) * 512].rearrange("c p t -> p c t"), ycs[w][:],
                       reads=[ycsb[w]], writes=[ycTb])
                if dbg and t8 == 0:
                    dump("cv", cv2[w][:], [128, 8, 512], BF16, [cv2b[w]])
                    dump("ycs", ycs[w][:], [128, 8, 512], BF16, [ycsb[w]])

            conv_blocks(0, list(range(8)))
            for t8 in range(8):
                ln_part(t8)
                if t8 + 1 < 8:
                    conv_blocks(t8 + 1, [0, 1, 2])
                yconv(t8)
                if t8 + 1 < 8:
                    conv_blocks(t8 + 1, [3, 4, 5, 6, 7])
            kb.barrier()

    if 4 in phases:
        with contextlib.ExitStack() as st4:
            extra_stg(st4, 6, "p4")
            wdo = kb.sb("wdo", [128, 8, D], BF16, st4)
            wdob = Buf("wdo")
            for hh in range(8):
                for n0 in (0, 512):
                    s = stg_i[0] % len(stg)
                    stg_i[0] += 1
                    kb.dma("sp", stg[s][:], wdo_d[:, hh, n0:n0 + 512], writes=[stgb[s]])
                    kb.op("dve", lambda e: e.tensor_scalar(wdo[:, hh, n0:n0 + 512], stg[s][:], pv[:, P_DNG:P_DNG + 1], None, ALU.mult),
                          reads=[stgb[s], pvb], writes=[wdob])
            wo = kb.sb("wo", [128, 8, D], BF16, st4)
            wob = Buf("wo")
            load_cast(wo, wob, wo_d, 8, D)
            ycl = [kb.sb("ycl%d" % i, [128, 8, 512], BF16, st4) for i in range(2)]
            yclb = [Buf("ycl%d" % i) for i in range(2)]
            otl = [kb.sb("otl%d" % i, [128, 8, 512], BF16, st4) for i in range(2)]
            otlb = [Buf("otl%d" % i) for i in range(2)]
            gl = [kb.sb("gl%d" % i, [128, 16, 512], BF16, st4) for i in range(2)]
            glb = [Buf("gl%d" % i) for i in range(2)]
            m1 = [kb.sb("m1_%d" % i, [128, 512], F32, st4) for i in range(2)]
            m1b = [Buf("m1_%d" % i) for i in range(2)]
            m2 = [kb.sb("m2_%d" % i, [128, 512], F32, st4) for i in range(2)]
            m2b = [Buf("m2_%d" % i) for i in range(2)]
            yT2 = [kb.sb("yT%d" % i, [128, 8, 512], BF16, st4) for i in range(2)]
            yT2b = [Buf("yT%d" % i) for i in range(2)]
            xt = [kb.sb("xt4_%d" % i, [128, D], F32, st4) for i in range(2)]
            xtb = [Buf("xt4_%d" % i) for i in range(2)]

            def wo_unit(t8, unit):
                yT, yTb = yT2[t8 % 2], yT2b[t8 % 2]
                sub, half = unit // 2, unit % 2
                r0 = t8 * 512 + sub * 128
                xs_ = (t8 * 4 + sub) % 2
                if half == 0:
                    kb.dma("sp", xt[xs_][:], x_d[r0:r0 + 128, :], writes=[xtb[xs_]])
                bk = 6 + half
                kb.group("pe", [mm(banks[bk][:, :], yT[:, kbk, sub * 128:(sub + 1) * 128], wo[:, kbk, half * 512:(half + 1) * 512],
                                   start=(kbk == 0), stop=(kbk == 7)) for kbk in range(8)],
                         reads=[yTb, wob], writes=[bb[bk]])
                kb.op("dve", lambda e: e.tensor_tensor(xt[xs_][:, half * 512:(half + 1) * 512], banks[bk][:, :],
                                                       xt[xs_][:, half * 512:(half + 1) * 512], ALU.add),
                      writes=[bb[bk], xtb[xs_]])
                if half == 1:
                    kb.dma("act", x1_d[r0:r0 + 128, :], xt[xs_][:], reads=[xtb[xs_]], writes=[x1b])

            for t8 in range(8):
                w = t8 % 2
                yT, yTb = yT2[w], yT2b[w]
                tsl = slice(t8 * 512, (t8 + 1) * 512)
                kb.dma("sp", ycl[w][:], ycT_d[:, :, tsl].rearrange("c p t -> p c t"), reads=[ycTb], writes=[yclb[w]])
                kb.dma("sp", otl[w][:], oT_d[:, :, tsl].rearrange("c p t -> p c t"), reads=oTb, writes=[otlb[w]])
                kb.dma("sp", gl[w][:], gT_d[:, :, tsl].rearrange("c p t -> p c t"), reads=[gTb], writes=[glb[w]])
                for nb in range(8):
                    s = nb % 2
                    b2 = nb % 4
                    kb.group("pe", [mm(banks[b2][:, :], wdo[:, hh, nb * 128:(nb + 1) * 128], otl[w][:, hh, :],
                                       start=(hh == 0), stop=(hh == 7)) for hh in range(8)],
                             reads=[wdob, otlb[w]], writes=[bb[b2]])
                    kb.op("dve", lambda e: e.tensor_tensor(m1[s][:], gl[w][:, nb, :], ycl[w][:, nb, :], ALU.mult),
                          reads=[glb[w], yclb[w]], writes=[m1b[s]])
                    kb.op("dve", lambda e: e.tensor_tensor(m2[s][:], banks[b2][:, :], gl[w][:, 8 + nb, :], ALU.mult),
                          reads=[glb[w]], writes=[bb[b2], m2b[s]])
                    kb.op("pool", lambda e: e.tensor_tensor(yT[:, nb, :], m1[s][:], m2[s][:], ALU.add),
                          reads=[m1b[s], m2b[s]], writes=[yTb])
                    if t8 > 0:
                        wo_unit(t8 - 1, nb)
                if dbg and t8 == 0:
                    dump("yT", yT[:], [128, 8, 512], BF16, [yTb])
            for unit in range(8):
                wo_unit(7, unit)
            kb.barrier()

    if 5 in phases:
        with contextlib.ExitStack() as st5:
            extra_stg(st5, 4, "p5")
            wgu = kb.sb("wgu", [128, 8, 2 * DFF], BF16, st5)
            wgub = [Buf("wgu%d" % i) for i in range(11)]
            wgu_order = [5, 0, 6, 1, 7, 2, 8, 3, 9, 4, 10]
            wgu_done = []

            def wgu_need(n):
                while len(wgu_done) < min(n, len(wgu_order)):
                    ci = wgu_order[len(wgu_done)]
                    load_cast(wgu, wgub, wgu_d, 8, 2 * DFF, scale_col=P_GFFN, order=[ci])
                    wgu_done.append(ci)
            wgu_need(3)
            wdn = kb.sb("wdn", [128, NFB, D], BF16, st5)
            wdnb = [Buf("wdn%d" % i) for i in range(2)]
            wdn_loaded = [False]
            x1t = [kb.sb("x1t%d" % i, [128, 2, D], F32, st5) for i in range(2)]
            x1tb = [Buf("x1t%d" % i) for i in range(2)]
            h2T = [kb.sb("h2T%d" % i, [128, 8, 256], BF16, st5) for i in range(2)]
            h2Tb = [Buf("h2T%d" % i) for i in range(2)]
            aT = kb.sb("aT", [128, NFB, 256], BF16, st5)
            aTb = Buf("aT")
            sgt = [kb.sb("sgt%d" % i, [128, 256], F32, st5) for i in range(2)]
            sgtb = [Buf("sgt%d" % i) for i in range(2)]
            xn5 = [kb.sb("xn5_%d" % i, [128, D], BF16, st5) for i in range(2)]
            xn5b = [Buf("xn5_%d" % i) for i in range(2)]
            junk5 = [kb.sb("junk5_%d" % i, [128, D], BF16, st5) for i in range(2)]
            junk5b = [Buf("junk5_%d" % i) for i in range(2)]
            stat5 = [kb.sb("stat5_%d" % i, [128, 3 * NCH], F32, st5) for i in range(2)]
            stat5b = [Buf("stat5_%d" % i) for i in range(2)]
            for i in range(2):
                kb.op("pool", lambda e: e.memset(stat5[i][:], 0.0), writes=[stat5b[i]])
            def p5_load(t16):
                w = t16 % 2
                r0 = t16 * 256
                kb.dma("sp", x1t[w][:], x1_d[r0:r0 + 256, :].rearrange("(s p) d -> p s d", p=128), reads=[x1b], writes=[x1tb[w]])

            def p5_rms(t16, part):
                w = t16 % 2
                for sub in range(2):
                    it = t16 * 2 + sub
                    rms_to_featmajor(x1t[w][:, sub, :], x1tb[w], h2T[w], h2Tb[w], sub * 128,
                                     (junk5[sub], junk5b[sub], stat5[sub], stat5b[sub], xn5[sub], xn5b[sub], 6 + sub), it, part=part)

            def p5_gate_up(t16):
                w = t16 % 2
                for fb in range(NFB):
                    bk = fb % 4
                    s = fb % 2
                    if t16 == 0:
                        wgu_need(3 + (fb + 1) // 2)
                    kb.group("pe", [mm(banks[bk][:, 0:256], wgu[:, dc, fb * 128:(fb + 1) * 128], h2T[w][:, dc, :],
                                       start=(dc == 0), stop=(dc == 7)) for dc in range(8)] +
                                   [mm(banks[bk][:, 256:512], wgu[:, dc, DFF + fb * 128:DFF + (fb + 1) * 128], h2T[w][:, dc, :],
                                       start=(dc == 0), stop=(dc == 7)) for dc in range(8)],
                             reads=[wgub[(fb * 128) // 512], wgub[(DFF + fb * 128) // 512], h2Tb[w]], writes=[bb[bk]])
                    kb.op("act", lambda e: e.activation(sgt[s][:], banks[bk][:, 0:256], AF.Silu), writes=[bb[bk], sgtb[s]])
                    kb.op("dve", lambda e: e.tensor_tensor(aT[:, fb, :], banks[bk][:, 256:512], sgt[s][:], ALU.mult),
                          reads=[sgtb[s]], writes=[bb[bk], aTb])

            def p5_down(t16):
                w = t16 % 2
                r0 = t16 * 256
                if not wdn_loaded[0]:
                    wgu_need(11)
                    load_cast(wdn, wdnb, wdn_d, NFB, D)
                    wdn_loaded[0] = True
                for sub in range(2):
                    for half in range(2):
                        bk = 4 + half
                        kb.group("pe", [mm(banks[bk][:, :], aT[:, fb, sub * 128:(sub + 1) * 128], wdn[:, fb, half * 512:(half + 1) * 512],
                                           start=(fb == 0), stop=(fb == NFB - 1)) for fb in range(NFB)],
                                 reads=[aTb, wdnb[half]], writes=[bb[bk]])
                        kb.op("dve", lambda e: e.tensor_tensor(x1t[w][:, sub, half * 512:(half + 1) * 512], banks[bk][:, :],
                                                               x1t[w][:, sub, half * 512:(half + 1) * 512], ALU.add),
                              writes=[bb[bk], x1tb[w]])
                kb.dma("pool", x2_d[r0:r0 + 256, :].rearrange("(s p) d -> p s d", p=128), x1t[w][:], reads=[x1tb[w]], writes=[x2b])

            p5_load(0)
            p5_load(1)
            p5_rms(0, None)
            for t16 in range(16):
                p5_gate_up(t16)
                if t16 + 1 < 16:
                    p5_rms(t16 + 1, "a")
                p5_down(t16)
                if t16 + 2 < 16:
                    p5_load(t16 + 2)
                if t16 + 1 < 16:
                    p5_rms(t16 + 1, "b")
            kb.barrier()

    if 6 in phases:
        with contextlib.ExitStack() as st6:
            extra_stg(st6, 6, "p6")
            wplg = kb.sb("wplg", [128, 8, D], BF16, st6)
            wplgb = Buf("wplg")
            load_cast(wplg, wplgb, wplg_d, 8, D, scale_col=P_GPL)
            wplp = kb.sb("wplp", [128, 2, D], BF16, st6)
            wplpb = Buf("wplp")
            load_cast(wplp, wplpb, wplp_d, 2, D)
            rv6 = kb.sb("rv6", [128, 2048], F32, st6)
            rv6b = Buf("rv6")
            kb.dma("sp", rv6[:], rvec_d[:, R_GPLP:R_GPLP + 2048].partition_broadcast(128), writes=[rv6b])
            gplp_bc = rv6[:, 0:1024]
            gfin_bc = rv6[:, 1024:2048]
            NS6 = 8
            def mk6(nm, shape, dt):
                return [kb.sb("%s%d" % (nm, i), shape, dt, st6) for i in range(NS6)], [Buf("%s%d" % (nm, i)) for i in range(NS6)]
            x2t, x2tb = mk6("x2t", [128, D], F32)
            pt, ptb = mk6("pt", [128, PLE], F32)
            pbf, pbfb = mk6("pbf", [128, PLE], BF16)
            pT, pTb = mk6("pT", [128, 2, 128], BF16)
            h3T, h3Tb = mk6("h3T", [128, 8, 128], BF16)
            tg6, tg6b = mk6("tg6", [128, D], F32)
            pl6, pl6b = mk6("pl6", [128, D], F32)
            xn6, xn6b = mk6("xn6", [128, D], BF16)
            junk6 = kb.sb("junk6", [128, D], BF16, st6)
            junk6b = Buf("junk6")
            sts = [kb.sb("sts6_%d" % i, [128, 24], F32, st6) for i in range(2)]
            stsb = [[Buf("sts6_%d_%d" % (i, k)) for k in range(3)] for i in range(2)]
            def ctx6(grp):
                gp = grp % 2
                tl = [(grp * 4 + i, (grp * 4 + i) % NS6, i) for i in range(4)]
                return sts[gp], stsb[gp], tl

            def front6(grp, inter):
                stt, (sbx, sbp, sbf), tl = ctx6(grp)
                kb.op("pool", lambda e: e.memset(stt[:, 0:8], 0.0), writes=[sbx])
                kb.op("pool", lambda e: e.memset(stt[:, 8:16], 0.0), writes=[sbp])
                kb.op("pool", lambda e: e.memset(stt[:, 16:24], 0.0), writes=[sbf])
                for tt, w, i in tl:
                    r0 = tt * 128
                    kb.dma("sp", x2t[w][:], x2_d[r0:r0 + 128, :], reads=[x2b], writes=[x2tb[w]])
                    kb.dma("sp", pt[w][:], p_d[r0:r0 + 128, :], writes=[ptb[w]])
                for tt, w, i in tl:
                    kb.op("act", lambda e: e.activation(junk6[:], x2t[w][:], AF.Square, scale=1.0 / 32.0, accum_out=stt[:, i:i + 1]),
                          reads=[x2tb[w]], writes=[junk6b, sbx])
                kb.op("act", lambda e: e.activation(stt[:, 4:8], stt[:, 0:4], AF.Ln, bias=epsc[:, 0:1]), reads=[epsb], writes=[sbx])
                kb.op("act", lambda e: e.activation(stt[:, 4:8], stt[:, 4:8], AF.Exp, scale=-0.5), writes=[sbx])
                for tt, w, i in tl:
                    kb.op("dve", lambda e: e.tensor_scalar(xn6[w][:], x2t[w][:], stt[:, 4 + i:5 + i], None, ALU.mult),
                          reads=[x2tb[w], sbx], writes=[xn6b[w]])
                    kb.op("pool", lambda e: e.tensor_copy(pbf[w][:], pt[w][:]), reads=[ptb[w]], writes=[pbfb[w]])
                for tt, w, i in tl:
                    bk = 4 + i
                    pv16 = bank16(bk, 0, 512)
                    kb.group("pe", [(lambda e, c=c: e.transpose(pv16[:, c * 128:(c + 1) * 128], xn6[w][:, c * 128:(c + 1) * 128], identb))
                                    for c in range(8)], reads=[xn6b[w], cbb], writes=[bb[bk]])
                for tt, w, i in tl:
                    bk = 4 + i
                    pv16 = bank16(bk, 0, 512)
                    if i % 2 == 0:
                        kb.op("dve", lambda e: e.tensor_copy(h3T[w][:], pv16.rearrange("p (c t) -> p c t", c=8)), writes=[bb[bk], h3Tb[w]])
                    else:
                        kb.op("act", lambda e: e.copy(h3T[w][:], pv16.rearrange("p (c t) -> p c t", c=8)), writes=[bb[bk], h3Tb[w]])
                for tt, w, i in tl:
                    bk = 4 + i
                    p16 = bank16(bk, 0, 128)
                    kb.group("pe", [(lambda e, c=c: e.transpose(p16[:, c * 128:(c + 1) * 128], pbf[w][:, c * 128:(c + 1) * 128], identb))
                                    for c in range(2)], reads=[pbfb[w], cbb], writes=[bb[bk]])
                for tt, w, i in tl:
                    bk = 4 + i
                    p16 = bank16(bk, 0, 128)
                    kb.op("dve", lambda e: e.tensor_copy(pT[w][:], p16.rearrange("p (c t) -> p c t", c=2)), writes=[bb[bk], pTb[w]])
                u = 0
                for tt, w, i in tl:
                    for half in range(2):
                        hs = slice(half * 512, (half + 1) * 512)
                        bg, bp = 2 * (u % 2), 2 * (u % 2) + 1
                        kb.group("pe", [mm(banks[bg][:, :], h3T[w][:, dc, :], wplg[:, dc, hs], start=(dc == 0), stop=(dc == 7))
                                        for dc in range(8)], reads=[h3Tb[w], wplgb], writes=[bb[bg]])
                        kb.group("pe", [mm(banks[bp][:, :], pT[w][:, kc, :], wplp[:, kc, hs], start=(kc == 0), stop=(kc == 1))
                                        for kc in range(2)], reads=[pTb[w], wplpb], writes=[bb[bp]])
                        kb.op("act", lambda e: e.activation(tg6[w][:, hs], banks[bg][:, :], AF.Tanh, scale=0.5), writes=[bb[bg], tg6b[w]])
                        kb.op("act", lambda e: e.copy(pl6[w][:, hs], banks[bp][:, :]), writes=[bb[bp], pl6b[w]])
                        if inter[u] is not None:
                            inter[u]()
                        u += 1
                    kb.op("act", lambda e: e.activation(junk6[:], pl6[w][:], AF.Square, scale=1.0 / 32.0, accum_out=stt[:, 8 + i:9 + i]),
                          reads=[pl6b[w]], writes=[junk6b, sbp])

            def back6_parts(grp):
                stt, (sbx, sbp, sbf), tl = ctx6(grp)

                def E():
                    kb.op("act", lambda e: e.activation(stt[:, 12:16], stt[:, 8:12], AF.Ln, bias=epsc[:, 0:1]), reads=[epsb], writes=[sbp])
                    kb.op("act", lambda e: e.activation(stt[:, 12:16], stt[:, 12:16], AF.Exp, scale=-0.5), writes=[sbp])
                    kb.op("dve", lambda e: e.tensor_scalar(stt[:, 12:16], stt[:, 12:16], 0.5, None, ALU.mult), writes=[sbp])

                def F(k):
                    tt, w, i = tl[k]
                    kb.op("pool", lambda e: e.tensor_tensor(pl6[w][:], pl6[w][:], gplp_bc, ALU.mult), reads=[rv6b], writes=[pl6b[w]])
                    kb.op("dve", lambda e: e.scalar_tensor_tensor(tg6[w][:], tg6[w][:], 1.0, pl6[w][:], ALU.add, ALU.mult),
                          reads=[pl6b[w]], writes=[tg6b[w]])
                    kb.op("dve", lambda e: e.scalar_tensor_tensor(x2t[w][:], tg6[w][:], stt[:, 12 + i:13 + i], x2t[w][:], ALU.mult, ALU.add),
                          reads=[tg6b[w], sbp], writes=[x2tb[w]])
                    kb.op("act", lambda e: e.activation(junk6[:], x2t[w][:], AF.Square, scale=1.0 / 32.0, accum_out=stt[:, 16 + i:17 + i]),
                          reads=[x2tb[w]], writes=[junk6b, sbf])

                def Gs():
                    kb.op("act", lambda e: e.activation(stt[:, 20:24], stt[:, 16:20], AF.Ln, bias=epsc[:, 0:1]), reads=[epsb], writes=[sbf])
                    kb.op("act", lambda e: e.activation(stt[:, 20:24], stt[:, 20:24], AF.Exp, scale=-0.5), writes=[sbf])

                def H(k):
                    tt, w, i = tl[k]
                    r0 = tt * 128
                    kb.op("dve", lambda e: e.scalar_tensor_tensor(tg6[w][:], x2t[w][:], stt[:, 20 + i:21 + i], gfin_bc, ALU.mult, ALU.mult),
                          reads=[x2tb[w], sbf, rv6b], writes=[tg6b[w]])
                    kb.dma("pool", out_d[r0:r0 + 128, :], tg6[w][:], reads=[tg6b[w]], writes=[outb])
                return E, F, Gs, H

            NG6 = NCH // 4
            for grp in range(NG6):
                if grp == 0:
                    inter = [None] * 8
                else:
                    E, F, Gs, H = back6_parts(grp - 1)
                    E()
                    def mkF(k, F=F, Gs=Gs):
                        def f():
                            F(k)
                            if k == 3:
                                Gs()
                        return f
                    inter = [mkF(k) for k in range(4)] + [(lambda k=k, H=H: H(k)) for k in range(4)]
                front6(grp, inter)
            E, F, Gs, H = back6_parts(NG6 - 1)
            E()
            for k in range(4):
                F(k)
            Gs()
            for k in range(4):
                H(k)
            kb.barrier()

    for b in list(dbg_outs.values()) + [outb]:
        kb._wait("sp", b.writer)
    return nc, kb


def host_prep(inputs):
    f = np.float32
    w_in = np.asarray(inputs["w_in"][0], f)

    def pk(w, kc):
        return np.ascontiguousarray(w.reshape(kc, 128, w.shape[1]).transpose(1, 0, 2))

    w_heads = np.empty((NH, 128, 8, 512), f)
    for h in range(NH):
        for j, off in enumerate((O_Q, O_K, O_V, O_Z)):
            w_heads[h, :, :, j * 128:(j + 1) * 128] = pk(w_in[:, off + h * 128:off + (h + 1) * 128], 8)
    shared = {
        "w_heads": w_heads,
        "w_ab": pk(w_in[:, O_A:O_A + 32], 8),
        "w_glu": pk(w_in[:, 0:2048], 8),
        "w_gate": pk(w_in[:, O_GA:O_GA + 2048], 8),
        "w_conv_out": pk(np.asarray(inputs["w_conv_out"][0], f), 8),
        "w_delta_out": pk(np.asarray(inputs["w_delta_out"][0], f), 8),
        "w_o": pk(np.asarray(inputs["w_o"][0], f), 8),
        "w_gate_up": pk(np.asarray(inputs["w_gate_up"][0], f), 8),
        "w_down": pk(np.asarray(inputs["w_down"][0], f), NFB),
        "w_pl_gate": pk(np.asarray(inputs["w_pl_gate"][0], f), 8),
        "w_pl_proj": pk(np.asarray(inputs["w_pl_proj"][0], f), 2),
    }
    cs = np.zeros((128, NCONST), f)
    i = np.arange(128)
    P, Fr = i[:, None], i[None, :]
    cs[:, C_ID:C_ID + 128] = (P == Fr)
    cs[:, C_TL:C_TL + 128] = (P <= Fr)
    cs[:, C_TG:C_TG + 128] = (P >= Fr)
    cs[:, C_ONE:C_ONE + 128] = 1.0
    sgf = np.where(P > Fr, -1.0, np.where(P < Fr, 1.0, 0.0))
    cs[:, C_SGF:C_SGF + 128] = sgf
    cs[:, C_SGB:C_SGB + 128] = -sgf
    cs[:, C_NEG:C_NEG + 128] = -1.0
    cs[:, C_MUP:C_MUP + 128] = (P <= Fr)
    cs[:, C_MLO:C_MLO + 128] = (P >= Fr)
    for l in range(7):
        b = 1 << l
        m = ((P // (2 * b)) == (Fr // (2 * b))) & ((P % (2 * b)) >= b) & ((Fr % (2 * b)) < b)
        cs[:, C_LV + l * 256:C_LV + l * 256 + 128] = -m.astype(f)
        cs[:, C_LV + l * 256 + 128:C_LV + (l + 1) * 256] = -m.T.astype(f)
        if l == 6:
            cs[:, C_LV6X:C_LV6X + 128] = -m.T.astype(f)
            cs[:, C_LV6X + 128:C_LV6X + 256] = -m.astype(f)
        if l in POOL_LV:
            o_ = C_LVQ + POOL_LV.index(l) * 512
            for bi, mk_ in enumerate((m, m.T, m.T, m)):
                cs[:, o_ + bi * 128:o_ + (bi + 1) * 128] = -mk_.astype(f)
    shared["consts"] = cs

    def pcol(v):
        return np.asarray(v, f).reshape(8, 128).T

    pvec = np.zeros((128, NP), f)
    pvec[:, P_GMIX:P_GMIX + 8] = pcol(inputs["g_mix"][0])
    pvec[:, P_GFFN:P_GFFN + 8] = pcol(inputs["g_ffn"][0])
    pvec[:, P_GPL:P_GPL + 8] = pcol(inputs["g_pl"][0])
    pvec[:, P_CB:P_CB + 8] = pcol(inputs["conv_dw_b"][0])
    pvec[:, P_LNG:P_LNG + 8] = pcol(inputs["conv_ln_g"][0])
    pvec[:, P_LNB:P_LNB + 8] = pcol(inputs["conv_ln_b"][0])
    qw = np.asarray(inputs["qkv_conv_w"][0], f)
    pvec[:, P_QKVW:P_QKVW + 120] = qw.reshape(5, 24, 128).transpose(2, 1, 0).reshape(128, 120)
    cw_ = np.asarray(inputs["conv_dw_w"][0], f)
    pvec[:, P_CW:P_CW + 8 * CK] = cw_.reshape(CK, 8, 128).transpose(2, 1, 0).reshape(128, 8 * CK)
    pvec[:, P_DNG] = np.asarray(inputs["delta_norm_g"][0], f)
    shared["pvec"] = pvec
    rvec = np.zeros((1, NR), f)
    rvec[0, R_DTB:R_DTB + 16] = np.asarray(inputs["dt_bias"][0], f).reshape(16)
    rvec[0, R_ALOG:R_ALOG + 16] = np.asarray(inputs["a_log"][0], f).reshape(16)
    rvec[0, R_DNG:R_DNG + 128] = np.asarray(inputs["delta_norm_g"][0], f)
    rvec[0, R_GPLP:R_GPLP + 1024] = np.asarray(inputs["g_pl_proj"][0], f)
    rvec[0, R_GFIN:R_GFIN + 1024] = np.asarray(inputs["g_final"], f)
    shared["rvec"] = rvec
    return shared


def kernel(**inputs):
    shared = host_prep(inputs)
    x = np.asarray(inputs["x"], np.float32)
    p = np.asarray(inputs["p"], np.float32)[0]
    nc, kb = build()
    in_maps = []
    for c in range(8):
        m = dict(shared)
        m["x"] = np.ascontiguousarray(x[c])
        m["p"] = np.ascontiguousarray(p[c])
        in_maps.append(m)
    res = run_bass_kernel_spmd(nc, in_maps, core_ids=list(range(8)))
    return np.stack([np.asarray(r["out"], np.float32) for r in res.results], axis=0)
```
